# Optimizing a Trainium2 kernel written in Bass

```python
import math
import jax
import jax.numpy as jnp
from jax import lax
import numpy as np

D_MODEL = 1024
BATCH = 1
SEQ = 16384
DEPTH = 2
DEC_BATCH = 8
DEC_SEQ = 4096
PAST_LEN = 128

N_BRANCH = 4
BRANCH_WIDTH = D_MODEL // N_BRANCH
HEAD_DIM = 64
EPS = 1e-6
A_HEADS = BRANCH_WIDTH // HEAD_DIM
A_CHUNK = 64
CONV_WIDTH = 5
B_HEADS = BRANCH_WIDTH // HEAD_DIM
DILATED_PATTERNS = ((128, 1), (512, 4), (2048, 16))
B_MAX_BLOCK = 128
S5_GROUP_CH = 16
S5_GROUPS = BRANCH_WIDTH // S5_GROUP_CH
S5_STATE = 64
D_HEADS = BRANCH_WIDTH // HEAD_DIM
D_KV_HEADS = D_HEADS // 2
D_RADIUS = 128
D_BLOCK = 128
FFN_HIDDEN = -(-8 * D_MODEL // (3 * 256)) * 256
IN_SPLITS = (BRANCH_WIDTH, BRANCH_WIDTH, BRANCH_WIDTH, BRANCH_WIDTH, 2 * A_HEADS, 2 * A_HEADS,
             BRANCH_WIDTH, BRANCH_WIDTH, BRANCH_WIDTH,
             BRANCH_WIDTH,
             D_HEADS * HEAD_DIM, D_KV_HEADS * HEAD_DIM, D_KV_HEADS * HEAD_DIM)
IN_WIDTH = sum(IN_SPLITS)

kernel_name = 'hybrid_bidir_encoder'


def _rms_norm(x, w):
    x32 = x.astype(jnp.float32)
    y = x32 * lax.rsqrt(jnp.mean(x32 * x32, axis=-1, keepdims=True) + EPS)
    return (y * w.astype(jnp.float32)).astype(x.dtype)


def _l2norm(x):
    return x * lax.rsqrt(jnp.sum(x * x, axis=-1, keepdims=True) + EPS)


def _alibi_slopes():
    n = D_HEADS + B_HEADS
    s = 2.0 ** (-8.0 * np.arange(1, n + 1) / n)
    return jnp.asarray(s[:D_HEADS], jnp.float32), jnp.asarray(s[D_HEADS:], jnp.float32)


def _banded_attention(q, k, v, slopes, radius, dist_scale, block, sink=None):
    b, L, hq, dh = q.shape
    hkv = k.shape[2]
    grp = hq // hkv
    nb = L // block
    kw = block + 2 * radius
    pad = ((0, 0), (radius, radius), (0, 0), (0, 0))
    idx = (jnp.arange(nb) * block)[:, None] + jnp.arange(kw)[None, :]
    kb = jnp.pad(k, pad)[:, idx]
    vb = jnp.pad(v, pad)[:, idx]
    qb = q.reshape(b, nb, block, hkv, grp, dh)
    s = jnp.einsum('bnqhgd,bnkhd->bnhgqk', qb, kb) * (dh ** -0.5)
    rel = jnp.arange(kw)[None, :] - radius - jnp.arange(block)[:, None]
    key_pos = idx - radius
    valid = (jnp.abs(rel) <= radius)[None] & ((key_pos >= 0) & (key_pos < L))[:, None, :]
    dist = (dist_scale * jnp.abs(rel)).astype(jnp.float32)
    s = s - slopes.reshape(hkv, grp, 1, 1) * dist
    s = jnp.where(valid[None, :, None, None], s, -jnp.inf)
    m = jnp.max(s, axis=-1)
    if sink is not None:
        sink_l = sink.reshape(hkv, grp, 1)
        m = jnp.maximum(m, sink_l)
    p = jnp.exp(s - m[..., None])
    denom = jnp.sum(p, axis=-1)
    if sink is not None:
        denom = denom + jnp.exp(sink_l - m)
    o = jnp.einsum('bnhgqk,bnkhd->bnqhgd', p, vb) / jnp.moveaxis(denom, -1, 2)[..., None]
    lse = jnp.moveaxis(m + jnp.log(denom), -1, 2)
    return o.reshape(b, L, hq, dh), lse.reshape(b, L, hq)


def _gated_delta_chunked(q, k, v, g, beta):
    b, L, h, dk = q.shape
    dv = v.shape[-1]
    c = A_CHUNK
    n = L // c

    def chunk(t):
        return jnp.moveaxis(t.reshape((b, n, c, h) + t.shape[3:]), 3, 1)

    q, k, v, g, beta = chunk(q), chunk(k), chunk(v), chunk(g), chunk(beta)
    g_cum = jnp.cumsum(g, axis=-1)
    incl = jnp.tril(jnp.ones((c, c), bool))
    strict = jnp.tril(jnp.ones((c, c), bool), -1)
    decay = jnp.exp(jnp.where(incl, g_cum[..., :, None] - g_cum[..., None, :], -jnp.inf))
    k_beta = k * beta[..., None]
    a_mat = jnp.where(strict, jnp.einsum('bhnid,bhnjd->bhnij', k_beta, k) * decay, 0.0)
    rhs = jnp.concatenate([v * beta[..., None], k_beta * jnp.exp(g_cum)[..., None]], axis=-1)
    sol = lax.linalg.triangular_solve(a_mat, rhs, left_side=True, lower=True, unit_diagonal=True)
    u, w = sol[..., :dv], sol[..., dv:]
    qk = jnp.einsum('bhnid,bhnjd->bhnij', q, k) * decay
    q_dec = q * jnp.exp(g_cum)[..., None]
    k_dec = k * jnp.exp(g_cum[..., -1:] - g_cum)[..., None]
    g_last = jnp.exp(g_cum[..., -1])
    xs = tuple(jnp.moveaxis(t, 2, 0) for t in (q_dec, k_dec, u, w, qk, g_last))

    def step(state, inp):
        qd, kd, uc, wc, qkc, gl = inp
        v_new = uc - jnp.einsum('bhcd,bhde->bhce', wc, state)
        out = jnp.einsum('bhcd,bhde->bhce', qd, state) + jnp.einsum('bhcs,bhse->bhce', qkc, v_new)
        state = state * gl[..., None, None] + jnp.einsum('bhcd,bhce->bhde', kd, v_new)
        return state, out

    s0 = jnp.zeros((b, h, dk, dv), jnp.float32)
    _, o = lax.scan(step, s0, xs)
    return o.transpose(1, 0, 3, 2, 4).reshape(b, L, h, dv)


def _mixer_gdn(q, k, v, z, alpha, beta, conv_w, a_log, dt_bias, norm_w):
    b, L, _ = q.shape
    f32 = jnp.float32
    qkv = jnp.concatenate([q, k, v], axis=-1).astype(f32)
    ch = qkv.shape[-1]
    pad = CONV_WIDTH // 2
    qkv = lax.conv_general_dilated(qkv, conv_w.astype(f32)[:, None, :], window_strides=(1,),
                                   padding=[(pad, pad)], dimension_numbers=('NWC', 'WIO', 'NWC'),
                                   feature_group_count=ch)
    qkv = jax.nn.silu(qkv)
    q, k, v = jnp.split(qkv, 3, axis=-1)
    heads = lambda t: t.reshape(b, L, A_HEADS, HEAD_DIM)
    q = _l2norm(heads(q)) * (HEAD_DIM ** -0.5)
    k = _l2norm(heads(k))
    v = heads(v)
    alpha = alpha.astype(f32).reshape(b, L, 2, A_HEADS)
    beta = jax.nn.sigmoid(beta.astype(f32).reshape(b, L, 2, A_HEADS))
    g = -jnp.exp(a_log.astype(f32)) * jax.nn.softplus(alpha + dt_bias.astype(f32))
    flip = lambda t: jnp.flip(t, axis=1)
    fwd = _gated_delta_chunked(q, k, v, g[:, :, 0], beta[:, :, 0])
    bwd = flip(_gated_delta_chunked(flip(q), flip(k), flip(v), flip(g[:, :, 1]), flip(beta[:, :, 1])))
    o = fwd + bwd
    o = o * lax.rsqrt(jnp.mean(o * o, axis=-1, keepdims=True) + EPS) * norm_w.astype(f32)
    o = o * jax.nn.silu(heads(z.astype(f32)))
    return o.reshape(b, L, BRANCH_WIDTH)


def _stride_gather(t, dil):
    b, L = t.shape[:2]
    return t.reshape((b, L // dil, dil) + t.shape[2:]).swapaxes(1, 2).reshape((b * dil, L // dil) + t.shape[2:])


def _stride_scatter(t, b):
    bd, ld = t.shape[:2]
    dil = bd // b
    return t.reshape((b, dil, ld) + t.shape[2:]).swapaxes(1, 2).reshape((b, ld * dil) + t.shape[2:])


def _mixer_dilated(q, k, v, slopes):
    b, L, _ = q.shape
    heads = lambda t: t.astype(jnp.float32).reshape(b, L, B_HEADS, HEAD_DIM)
    q, k, v = heads(q), heads(k), heads(v)
    outs, lses = [], []
    for window, dil in DILATED_PATTERNS:
        ld = L // dil
        o, lse = _banded_attention(_stride_gather(q, dil), _stride_gather(k, dil), _stride_gather(v, dil),
                                   slopes, window // (2 * dil), dil, math.gcd(ld, B_MAX_BLOCK))
        outs.append(_stride_scatter(o, b))
        lses.append(_stride_scatter(lse, b))
    wts = jax.nn.softmax(jnp.stack(lses, axis=-1), axis=-1)
    o = jnp.einsum('blhp,pblhd->blhd', wts, jnp.stack(outs, axis=0))
    return o.reshape(b, L, BRANCH_WIDTH)


def _cmul(ar, ai, br, bi):
    return ar * br - ai * bi, ar * bi + ai * br


def _ssm_combine(e1, e2):
    a1r, a1i, b1r, b1i = e1
    a2r, a2i, b2r, b2i = e2
    ar, ai = _cmul(a2r, a2i, a1r, a1i)
    br, bi = _cmul(a2r, a2i, b1r, b1i)
    return ar, ai, br + b2r, bi + b2i


def _mixer_s5(u, lam_re, lam_im, log_step, b_re, b_im, c_re, c_im, d_skip, w_glu):
    f32 = jnp.float32
    b, L, _ = u.shape
    u = u.astype(f32)
    ug = u.reshape(b, L, S5_GROUPS, S5_GROUP_CH)
    bu_re = jnp.einsum('blgc,gpc->blgp', ug, b_re.astype(f32))
    bu_im = jnp.einsum('blgc,gpc->blgp', ug, b_im.astype(f32))
    y = u * d_skip.astype(f32)
    for direction in range(2):
        lr = lam_re[direction].astype(f32)
        li = lam_im[direction].astype(f32)
        step = jnp.exp(log_step[direction].astype(f32))[:, None]
        mag = jnp.exp(lr * step)
        ab_re, ab_im = mag * jnp.cos(li * step), mag * jnp.sin(li * step)
        inv = 1.0 / (lr * lr + li * li)
        coef_re, coef_im = _cmul(ab_re - 1.0, ab_im, lr * inv, -li * inv)
        x_re, x_im = _cmul(coef_re, coef_im, bu_re, bu_im)
        a_re = jnp.broadcast_to(ab_re, x_re.shape)
        a_im = jnp.broadcast_to(ab_im, x_im.shape)
        _, _, s_re, s_im = lax.associative_scan(_ssm_combine, (a_re, a_im, x_re, x_im),
                                                reverse=(direction == 1), axis=1)
        yc = (jnp.einsum('blgp,gcp->blgc', s_re, c_re[direction].astype(f32))
              - jnp.einsum('blgp,gcp->blgc', s_im, c_im[direction].astype(f32)))
        y = y + yc.reshape(b, L, BRANCH_WIDTH)
    hg = jax.nn.gelu(y) @ w_glu.astype(f32)
    return hg[..., :BRANCH_WIDTH] * jax.nn.sigmoid(hg[..., BRANCH_WIDTH:])


def _mixer_window(q, k, v, slopes, sink):
    b, L, _ = q.shape
    f32 = jnp.float32
    q = q.astype(f32).reshape(b, L, D_HEADS, HEAD_DIM)
    k = k.astype(f32).reshape(b, L, D_KV_HEADS, HEAD_DIM)
    v = v.astype(f32).reshape(b, L, D_KV_HEADS, HEAD_DIM)
    o, _ = _banded_attention(q, k, v, slopes, D_RADIUS, 1, D_BLOCK, sink.astype(f32))
    return o.reshape(b, L, BRANCH_WIDTH)


def _trunk(x, params):
    (norm1, w_in, conv_a, a_log, dt_bias, a_norm,
     s5_lam_re, s5_lam_im, s5_log_step, s5_b_re, s5_b_im, s5_c_re, s5_c_im, s5_d, s5_w_glu,
     attn_sink, w_gate, w_branch, w_out, norm2, w_ffn_gate, w_ffn_up, w_ffn_down, final_norm) = params
    slopes_d, slopes_b = _alibi_slopes()
    split_points = np.cumsum(IN_SPLITS)[:-1].tolist()
    for l in range(DEPTH):
        h = _rms_norm(x, norm1[l])
        (a_q, a_k, a_v, a_z, a_alpha, a_beta, b_q, b_k, b_v, c_u, d_q, d_k, d_v) = jnp.split(
            h @ w_in[l], split_points, axis=-1)
        branches = (
            _mixer_gdn(a_q, a_k, a_v, a_z, a_alpha, a_beta, conv_a[l], a_log[l], dt_bias[l], a_norm[l]),
            _mixer_dilated(b_q, b_k, b_v, slopes_b),
            _mixer_s5(c_u, s5_lam_re[l], s5_lam_im[l], s5_log_step[l], s5_b_re[l], s5_b_im[l],
                      s5_c_re[l], s5_c_im[l], s5_d[l], s5_w_glu[l]),
            _mixer_window(d_q, d_k, d_v, slopes_d, attn_sink[l]),
        )
        merged = jnp.zeros_like(x)
        for i, o in enumerate(branches):
            gate = jax.nn.sigmoid(h @ w_gate[l, i])
            merged = merged + gate * (o.astype(x.dtype) @ w_branch[l, i])
        x = x + merged @ w_out[l]
        h = _rms_norm(x, norm2[l])
        x = x + (jax.nn.silu(h @ w_ffn_gate[l]) * (h @ w_ffn_up[l])) @ w_ffn_down[l]
    return _rms_norm(x, final_norm)


def setup_inputs(seed: int = 0) -> dict:
    key = jax.random.key(seed)
    ks = jax.random.split(key, 26)
    f32 = jnp.float32
    nrm = lambda k, shape, scale: scale * jax.random.normal(k, shape, f32)
    dt = jnp.exp(jax.random.uniform(ks[5], (DEPTH, 2, A_HEADS), f32, math.log(1e-3), math.log(1e-1)))
    return {
        'x_prompt': nrm(ks[0], (BATCH, SEQ, D_MODEL), 1.0),
        'x_sample': nrm(ks[1], (DEC_BATCH, DEC_SEQ, D_MODEL), 1.0),
        'norm1': 1.0 + nrm(ks[2], (DEPTH, D_MODEL), 0.02),
        'w_in': nrm(ks[3], (DEPTH, D_MODEL, IN_WIDTH), D_MODEL ** -0.5),
        'conv_a': nrm(ks[4], (DEPTH, CONV_WIDTH, 3 * BRANCH_WIDTH), CONV_WIDTH ** -0.5),
        'a_log': jnp.log(jax.random.uniform(ks[6], (DEPTH, 2, A_HEADS), f32, 1.0, 16.0)),
        'dt_bias': dt + jnp.log(-jnp.expm1(-dt)),
        'a_norm': 1.0 + nrm(ks[7], (DEPTH, HEAD_DIM), 0.02),
        's5_lam_re': -0.5 + nrm(ks[8], (DEPTH, 2, S5_GROUPS, S5_STATE), 0.01),
        's5_lam_im': math.pi * jnp.arange(S5_STATE, dtype=f32) + nrm(ks[9], (DEPTH, 2, S5_GROUPS, S5_STATE), 0.01),
        's5_log_step': jax.random.uniform(ks[10], (DEPTH, 2, S5_GROUPS), f32, math.log(1e-3), math.log(1e-1)),
        's5_b_re': nrm(ks[11], (DEPTH, S5_GROUPS, S5_STATE, S5_GROUP_CH), (2 * S5_GROUP_CH) ** -0.5),
        's5_b_im': nrm(ks[12], (DEPTH, S5_GROUPS, S5_STATE, S5_GROUP_CH), (2 * S5_GROUP_CH) ** -0.5),
        's5_c_re': nrm(ks[13], (DEPTH, 2, S5_GROUPS, S5_GROUP_CH, S5_STATE), S5_STATE ** -0.5),
        's5_c_im': nrm(ks[14], (DEPTH, 2, S5_GROUPS, S5_GROUP_CH, S5_STATE), S5_STATE ** -0.5),
        's5_d': nrm(ks[15], (DEPTH, BRANCH_WIDTH), 1.0),
        's5_w_glu': nrm(ks[16], (DEPTH, BRANCH_WIDTH, 2 * BRANCH_WIDTH), BRANCH_WIDTH ** -0.5),
        'attn_sink': nrm(ks[17], (DEPTH, D_HEADS), 0.5),
        'w_gate': nrm(ks[18], (DEPTH, N_BRANCH, D_MODEL, D_MODEL), D_MODEL ** -0.5),
        'w_branch': nrm(ks[19], (DEPTH, N_BRANCH, BRANCH_WIDTH, D_MODEL), BRANCH_WIDTH ** -0.5),
        'w_out': nrm(ks[20], (DEPTH, D_MODEL, D_MODEL), D_MODEL ** -0.5),
        'norm2': 1.0 + nrm(ks[21], (DEPTH, D_MODEL), 0.02),
        'w_ffn_gate': nrm(ks[22], (DEPTH, D_MODEL, FFN_HIDDEN), D_MODEL ** -0.5),
        'w_ffn_up': nrm(ks[23], (DEPTH, D_MODEL, FFN_HIDDEN), D_MODEL ** -0.5),
        'w_ffn_down': nrm(ks[24], (DEPTH, FFN_HIDDEN, D_MODEL), FFN_HIDDEN ** -0.5),
        'final_norm': 1.0 + nrm(ks[25], (D_MODEL,), 0.02),
    }


def reference(x_prompt, x_sample, norm1, w_in, conv_a, a_log, dt_bias, a_norm,
              s5_lam_re, s5_lam_im, s5_log_step, s5_b_re, s5_b_im, s5_c_re, s5_c_im, s5_d, s5_w_glu,
              attn_sink, w_gate, w_branch, w_out, norm2, w_ffn_gate, w_ffn_up, w_ffn_down, final_norm):
    params = (norm1, w_in, conv_a, a_log, dt_bias, a_norm,
              s5_lam_re, s5_lam_im, s5_log_step, s5_b_re, s5_b_im, s5_c_re, s5_c_im, s5_d, s5_w_glu,
              attn_sink, w_gate, w_branch, w_out, norm2, w_ffn_gate, w_ffn_up, w_ffn_down, final_norm)
    y_prompt = _trunk(x_prompt, params)
    y_sample = _trunk(x_sample, params)
    return (y_prompt, y_sample)
```

```python
import contextlib
import numpy as np
import concourse.bass as bass
import concourse.mybir as mybir
from concourse.bass_utils import run_bass_kernel_spmd

F32 = mybir.dt.float32
BF16 = mybir.dt.bfloat16
AF = mybir.ActivationFunctionType
ALU = mybir.AluOpType

D = 1024
FF = 2816
NFF = FF // 128
EPS = 1e-6
TB = 256
HF = NFF // 2


class Prog:
    ENG = ("pe", "act", "dve", "pool", "sp")
    ROT = 30000

    def __init__(self, nc, stack):
        self.nc = nc
        self.stack = stack
        self.eng = {"pe": nc.tensor, "act": nc.scalar, "dve": nc.vector, "pool": nc.gpsimd, "sp": nc.sync}
        self.cnt = {e: 0 for e in self.ENG}
        self.sems = {}
        self.waited = {e: {} for e in self.ENG}
        self.lastw = {}
        self.readers = {}
        self.ops = {e: [] for e in self.ENG}
        self.slot_cnt = {}
        self.fence = {}
        self.fence_pending = {e: False for e in self.ENG}
        self.nsem = 0

    def sem(self, name):
        if name not in self.sems:
            self.sems[name] = self.stack.enter_context(self.nc.semaphore(name))
            self.nsem += 1
        return self.sems[name]

    def _deps(self, e, reads, writes):
        deps = {}

        def add(ev):
            if ev is None:
                return
            s, v, pe = ev
            if pe and e == "pe":
                return
            if deps.get(s, 0) < v:
                deps[s] = v
        for k in reads:
            add(self.lastw.get(k))
        for k in writes:
            add(self.lastw.get(k))
            for ev in self.readers.get(k, ()):
                add(ev)
        if self.fence_pending[e]:
            for s, v in self.fence.items():
                if deps.get(s, 0) < v:
                    deps[s] = v
            self.fence_pending[e] = False
        out = []
        for s, v in deps.items():
            if self.waited[e].get(s, 0) < v:
                self.waited[e][s] = v
                out.append((s, v))
        return out

    def _reg(self, ev, reads, writes):
        for k in reads:
            self.readers.setdefault(k, []).append(ev)
        for k in writes:
            self.lastw[k] = ev
            self.readers[k] = []

    def op(self, e, fn, reads=(), writes=()):
        waits = self._deps(e, reads, writes)
        self.cnt[e] += 1
        n = self.cnt[e]
        sname = "%s%d" % (e, n // self.ROT)
        self.sem(sname)
        val = n % self.ROT
        if val == 0:
            sname = "%s%d" % (e, n // self.ROT - 1)
            val = self.ROT
        ev = (sname, val, e == "pe")
        self._reg(ev, reads, writes)
        self.ops[e].append((fn, waits, (sname, 1)))
        self.fence[sname] = val

    def dma(self, e, slot, out, in_, reads=(), writes=()):
        waits = self._deps(e, reads, writes)
        sname = "d_" + slot
        self.sem(sname)
        self.slot_cnt[sname] = self.slot_cnt.get(sname, 0) + 16
        ev = (sname, self.slot_cnt[sname], False)
        self._reg(ev, reads, writes)
        self.ops[e].append((lambda eng, o=out, i=in_: eng.dma_start(out=o, in_=i), waits, (sname, 16)))
        self.fence[sname] = self.slot_cnt[sname]

    def barrier(self):
        for e in self.ENG:
            self.fence_pending[e] = True

    def emit(self):
        self.barrier()
        for e in self.ENG:
            waits = self._deps(e, (), ())
            self.ops[e].append((None, waits, None))
        with self.nc.Block() as block:
            def runner(e):
                def f(eng):
                    for fn, waits, inc in self.ops[e]:
                        for s, v in waits:
                            eng.wait_ge(self.sems[s], v)
                        if fn is not None:
                            fn(eng).then_inc(self.sems[inc[0]], inc[1])
                return f
            block.tensor(runner("pe"))
            block.scalar(runner("act"))
            block.vector(runner("dve"))
            block.gpsimd(runner("pool"))
            block.sync(runner("sp"))
        self.ops = {e: [] for e in self.ENG}


def build(LS, LP, depth=2, test_o=False, test_mix=False, mixers=(), lean=False, dense_mode=False):
    nc = bass.Bass("TRN2", target_bir_lowering=False)
    seqs = [("s", LS), ("p", LP)]
    dt = lambda name, shape, dtype, kind="Internal": nc.dram_tensor(name, shape, dtype, kind=kind).ap()
    xin = {s: dt("x_" + s, [D, L], F32, "ExternalInput") for s, L in seqs}
    yout = {s: (None if test_mix else dt("y_" + s, [D, L], F32, "ExternalOutput")) for s, L in seqs}
    w_in = dt("w_in", [depth, D, 2576], F32, "ExternalInput")
    if test_mix:
        dt0 = dt
        dt = lambda *a, **k: None if (len(a) > 3 and a[3] == "ExternalInput" and a[0].startswith("w_") and a[0] != "w_in") else dt0(*a, **k)
    w_gate = dt("w_gate", [depth, 4, D, D], F32, "ExternalInput")
    w_branch = dt("w_branch", [depth, 4, 256, D], F32, "ExternalInput")
    w_out = dt("w_out", [depth, D, D], F32, "ExternalInput")
    w_fg = dt("w_ffn_gate", [depth, D, FF], F32, "ExternalInput")
    w_fu = dt("w_ffn_up", [depth, D, FF], F32, "ExternalInput")
    w_fd = dt("w_ffn_down", [depth, FF, D], F32, "ExternalInput")
    nrm = dt("norms", [128, (2 * depth + 1) * 8], F32, "ExternalInput")
    xbuf = {s: [dt("xb%d_%s" % (i, s), [D, L], F32, "ExternalOutput" if (test_o and (i == 1 or not dense_mode)) else "Internal") for i in range(3)] for s, L in seqs}
    obuf = {s: dt("ob_" + s, [D, L], BF16, "ExternalInput" if test_o else ("ExternalOutput" if test_mix else "Internal")) for s, L in seqs}
    pbufA = {s: dt("pa_" + s, [1280, L], F32, "ExternalOutput" if (test_mix and not lean) else "Internal") for s, L in seqs}
    pbufB = {s: dt("pb_" + s, [1280, L], BF16, "ExternalOutput" if (test_mix and not lean) else "Internal") for s, L in seqs}
    abuf = {s: dt("ab_" + s, [L, 16], F32, "ExternalOutput" if (test_mix and not lean) else "Internal") for s, L in seqs}
    wtab_d = dt("wtab", [128, 9 * 512], F32, "ExternalInput")
    ident_d = dt("ident", [128, 128], BF16, "ExternalInput")
    sink_d = dt("sinkb", [64, depth * 4], F32, "ExternalInput")
    s5par_d = dt("s5par", [128, depth * 48], F32, "ExternalInput")
    s5B_d = dt("s5B", [depth, 128, 8 * 2 * 128], F32, "ExternalInput")
    s5C_d = dt("s5C", [depth, 128, 2 * 8 * 2 * 128], F32, "ExternalInput")
    s5d_d = dt("s5d", [128, depth * 2], F32, "ExternalInput")
    wglu_d = dt("s5_w_glu", [depth, 256, 512], F32, "ExternalInput")
    ybuf = {s: dt("yb_" + s, [256, L], F32) for s, L in seqs}
    ofwd = {s: dt("of_" + s, [L, 256], F32, "ExternalOutput" if (test_mix and not lean) else "Internal") for s, L in seqs}
    dbg_d = dt("dbg", [128, 16 * 128], F32, "ExternalOutput") if (test_mix and not lean) else None
    gmask_d = dt("gmask", [128, 8 * 128], F32, "ExternalInput")
    gconv_d = dt("gconv", [128, depth * 30], F32, "ExternalInput")
    gpar_d = dt("gpar", [128, depth * 16], F32, "ExternalInput")
    gnorm_d = dt("gnorm", [128, depth * 256], F32, "ExternalInput")

    stack = contextlib.ExitStack()
    with stack:
        P = Prog(nc, stack)
        sb = lambda name, shape, dtype: stack.enter_context(nc.sbuf_tensor(name, shape, dtype))
        ps = lambda name, shape, dtype: stack.enter_context(nc.psum_tensor(name, shape, dtype))
        ones = sb("ones", [128, 128], F32)
        nrm_t = sb("nrm_t", [128, (2 * depth + 1) * 8], F32)
        P.op("pool", lambda e: e.memset(ones[:], 1.0), writes=["ones"])
        P.dma("sp", "c0", nrm_t[:], nrm[:, :], writes=["nrm"])
        ident = sb("ident_t", [128, 128], BF16)
        P.dma("sp", "c0", ident[:], ident_d[:, :], writes=["ident"])
        sinke = sb("sinke", [64, depth * 4], F32)
        P.dma("sp", "c0", sinke[:], sink_d[:, :], writes=["sinke"])
        P.op("act", lambda e: e.activation(out=sinke[:], in_=sinke[:], func=AF.Exp), reads=["sinke"], writes=["sinke"])
        epsb = sb("epsb", [128, 1], F32)
        oneb = sb("oneb", [128, 1], F32)
        P.op("pool", lambda e: e.memset(epsb[:], EPS), writes=["epsb"])
        P.op("pool", lambda e: e.memset(oneb[:], 1.0), writes=["oneb"])
        sel = sb("sel", [128, 64], F32)
        P.op("pool", lambda e: e.memset(sel[:], 0.0), writes=["sel"])
        P.op("pool", lambda e: e.memset(sel[64:65, :], 1.0), reads=["sel"], writes=["sel"])
        psum = [ps("ps%d" % i, [128, 512], F32) for i in range(8)]
        ARENA = 44672
        arena = sb("arena", [128, ARENA], F32)
        apos = [0]

        def carve(shape, dtype):
            n = int(np.prod(shape[1:]))
            nf = n if dtype == F32 else (n + 1) // 2
            v = arena[:, apos[0]:apos[0] + nf]
            apos[0] += nf
            assert apos[0] <= ARENA, apos[0]
            if dtype != F32:
                v = v.bitcast(dtype)
            if len(shape) == 3:
                v = v.rearrange("p (a b) -> p a b", a=shape[1])
            return v
        WMAX = 4 * 8 * D + 4 * 2 * D + 8 * D
        wres = carve([128, WMAX], BF16)
        wstage = [carve([128, 1024], F32) for i in range(2)]
        xt = [carve([128, 8, TB], F32) for i in range(2)]
        sq = carve([128, 8, TB], F32)
        rstd = carve([128, TB], F32)
        hT = carve([128, 8, TB], BF16)
        pos_hT = apos[0]
        mg = carve([128, 8, TB], BF16)
        ot = [carve([128, 8, TB], BF16) for i in range(2)]
        act = carve([128, HF, TB], BF16)
        sig = [carve([128, TB], F32) for i in range(2)]
        tmp = [carve([128, TB], F32) for i in range(2)]
        macc = carve([128, TB], F32)
        xo1 = carve([128, 8, TB], F32)
        xo = [xo1, xo1]
        xr1 = carve([128, 8, TB], F32)
        xr = [xr1, xr1]
        st = {"w": 0, "ps": 0, "q": 0}

        def load_w(dst_off, src, rows, cols):
            kc = rows // 128
            srcv = src.rearrange("(k p) c -> p k c", p=128)
            step = max(1, 1024 // cols)
            for k0 in range(0, kc, step):
                kn = min(step, kc - k0)
                for c0 in range(0, cols, 1024):
                    cn = min(1024, cols - c0)
                    i = st["w"] % 2
                    st["w"] += 1
                    if cols <= 1024:
                        dv = wstage[i][:, 0:kn * cols].rearrange("p (k c) -> p k c", k=kn)
                        sv = srcv[:, k0:k0 + kn, :]
                        ov = wres[:, dst_off + k0 * cols: dst_off + (k0 + kn) * cols].rearrange("p (k c) -> p k c", k=kn)
                    else:
                        dv = wstage[i][:, 0:cn]
                        sv = srcv[:, k0, c0:c0 + cn]
                        ov = wres[:, dst_off + k0 * cols + c0: dst_off + k0 * cols + c0 + cn]
                    q = ("sp", "act")[st["q"] % 2]
                    st["q"] += 1
                    P.dma(q, "ws%d" % i, dv, sv, writes=["wst%d" % i])
                    ce = ("dve", "pool")[i]
                    P.op(ce, lambda e, o=ov, a=dv: e.tensor_copy(out=o, in_=a), reads=["wst%d" % i], writes=["wres"])

        def next_ps():
            i = st["ps"] % 8
            st["ps"] += 1
            return i

        def rmsnorm(xtile, xkey, wcol, dst, dkey, n, same=False):
            for c in range(8):
                ce = ("act", "pool")[c % 2]
                if ce == "act":
                    P.op("act", lambda e, c=c: e.activation(out=sq[:, c, :n], in_=xtile[:, c, :n], func=AF.Square),
                         reads=[xkey], writes=["sq%d" % c])
                else:
                    P.op("pool", lambda e, c=c: e.tensor_tensor(out=sq[:, c, :n], in0=xtile[:, c, :n], in1=xtile[:, c, :n],
                                                                op=ALU.mult), reads=[xkey], writes=["sq%d" % c])
            pi = next_ps()
            for c in range(8):
                P.op("pe", lambda e, c=c, pi=pi: e.matmul(psum[pi][:, :n], lhsT=ones[:], rhs=sq[:, c, :n],
                                                          start=(c == 0), stop=(c == 7)),
                     reads=["sq%d" % c, "ones"], writes=["ps%d" % pi])
            P.op("dve", lambda e, pi=pi: e.tensor_scalar(out=rstd[:, :n], in0=psum[pi][:, :n], scalar1=1.0 / D, scalar2=EPS,
                                                         op0=ALU.mult, op1=ALU.add), reads=["ps%d" % pi], writes=["rstd"])
            P.op("act", lambda e: e.activation(out=rstd[:, :n], in_=rstd[:, :n], func=AF.Sqrt), reads=["rstd"], writes=["rstd"])
            P.op("dve", lambda e: e.reciprocal(out=rstd[:, :n], in_=rstd[:, :n]), reads=["rstd"], writes=["rstd"])
            for c in range(8):
                P.op("dve", lambda e, c=c: e.scalar_tensor_tensor(out=dst[:, c, :n], in0=xtile[:, c, :n],
                                                               scalar=nrm_t[:, wcol + c:wcol + c + 1], in1=rstd[:, :n],
                                                               op0=ALU.mult, op1=ALU.mult),
                     reads=[xkey, "rstd", "nrm"], writes=[dkey if same else dkey + str(c)])


        A_COLS = [0, 128, 256, 384, 512, 640, 768, 896, 1808, 1936]
        B_COLS = [1040, 1168, 1296, 1424, 1552, 1680, 2064, 2192, 2320, 2448]

        def stage_A(l):
            P.barrier()
            load_w(0, w_in[l], D, 2576)
            apos[0] = pos_hT
            stA = carve([128, 10, TB], F32)
            stB = carve([128, 10, TB], BF16)
            stab = carve([128, 2, 16], F32)
            for s, L in seqs:
                src = xin[s] if l == 0 else xbuf[s][1]
                for t0 in range(0, L, TB):
                    bi = (t0 // TB) % 2
                    xk = "xt%d" % bi
                    P.dma("sp", "x%d" % bi, xt[bi][:], src[:, t0:t0 + TB].rearrange("(c p) n -> p c n", p=128),
                          reads=["xsrc_" + s], writes=[xk])
                    rmsnorm(xt[bi], xk, (2 * l) * 8, hT, "hT", TB)
                    for (cols, stg, skey, is_a) in ((A_COLS, stA, "stA", True), (B_COLS, stB, "stB", False)):
                        for i, c0 in enumerate(cols):
                            pi = next_ps()
                            for k in range(8):
                                P.op("pe", lambda e, pi=pi, k=k, c0=c0: e.matmul(
                                    psum[pi][:, :TB], lhsT=wres[:, k * 2576 + c0: k * 2576 + c0 + 128], rhs=hT[:, k, :],
                                    start=(k == 0), stop=(k == 7)), reads=["hT%d" % k, "wres"], writes=["ps%d" % pi])
                            if i % 2 == 0:
                                P.op("act", lambda e, pi=pi, i=i, stg=stg: e.copy(out=stg[:, i, :], in_=psum[pi][:, :TB]),
                                     reads=["ps%d" % pi], writes=[skey])
                            else:
                                P.op("dve", lambda e, pi=pi, i=i, stg=stg: e.tensor_copy(out=stg[:, i, :], in_=psum[pi][:, :TB]),
                                     reads=["ps%d" % pi], writes=[skey])
                        dstbuf = (pbufA if is_a else pbufB)[s]
                        P.dma("pool" if is_a else "act", "sa" if is_a else "sb",
                              dstbuf[:, t0:t0 + TB].rearrange("(c p) n -> p c n", p=128), stg[:],
                              reads=[skey], writes=[("pa_" if is_a else "pb_") + s])
                    for hf in range(TB // 128):
                        pi = next_ps()
                        for k in range(8):
                            P.op("pe", lambda e, pi=pi, k=k, hf=hf: e.matmul(
                                psum[pi][:, 0:16], lhsT=hT[:, k, hf * 128:(hf + 1) * 128], rhs=wres[:, k * 2576 + 1024: k * 2576 + 1040],
                                start=(k == 0), stop=(k == 7)), reads=["hT%d" % k, "wres"], writes=["ps%d" % pi])
                        P.op("dve", lambda e, pi=pi, hf=hf: e.tensor_copy(out=stab[:, hf, :], in_=psum[pi][:, 0:16]),
                             reads=["ps%d" % pi], writes=["stab"])
                    P.dma("sp", "sab", abuf[s][t0:t0 + TB, :].rearrange("(h p) c -> p h c", p=128), stab[:],
                          reads=["stab"], writes=["ab_" + s])

        QS, HALO = 2048, 1024
        WIN = QS + 2 * HALO

        def attention(l):
            P.barrier()
            apos[0] = 0
            kB = carve([128, 2, WIN], BF16)
            vB = carve([128, 2, WIN], BF16)
            qB = carve([128, 4, QS], BF16)
            kD = carve([128, QS + 256], BF16)
            vD = carve([128, QS + 256], BF16)
            qD = carve([128, 4, QS], BF16)
            PATS = [(1, QS // 128 + 1), (4, QS // 4 // 128 + 1), (16, QS // 16 // 128 + 1)]
            tbase, nt = {}, 0
            for (dil, ntile) in PATS:
                for r in range(dil):
                    tbase[(dil, r)] = nt
                    nt += ntile
            vaB = carve([128, nt, 264], BF16)
            vaD = carve([128, QS // 128 + 2, 132], BF16)
            acc = carve([128, 4, QS], F32)
            wt = carve([128, 9, 512], F32)
            es = [carve([128, 512], F32) for _ in range(2)]
            pts = [carve([128, 512], BF16) for _ in range(6)]
            rec = es[0]
            ost = [pts[0], pts[1]]
            P.dma("sp", "c0", wt[:], wtab_d[:, :].rearrange("p (a b) -> p a b", a=9), writes=["wt"])
            P.op("pool", lambda e: e.memset(qB[:], 0.0), writes=["qB"])
            P.op("pool", lambda e: e.memset(qD[:], 0.0), writes=["qD"])
            cnt = {"es": 0, "ost": 0, "tr": 0}

            for s, L in seqs:
                for q0 in range(0, L, QS):
                    w0 = q0 - HALO
                    lo, hi = max(0, w0), min(L, q0 + QS + HALO)
                    for (buf, key, row0) in ((kB, "kB", 256), (vB, "vB", 512)):
                        if lo > w0:
                            P.op("pool", lambda e, buf=buf, n=lo - w0: e.memset(buf[:, :, 0:n], 0.0), writes=[key])
                        if hi < w0 + WIN:
                            P.op("pool", lambda e, buf=buf, a=hi - w0: e.memset(buf[:, :, a:WIN], 0.0), writes=[key])
                        P.dma("sp", "l" + key, buf[:, :, lo - w0:hi - w0],
                              pbufB[s][row0:row0 + 256, lo:hi].rearrange("(c p) n -> p c n", p=128), reads=["pb_" + s], writes=[key])
                    for h in range(4):
                        P.dma("act", "lqB", qB[(h % 2) * 64:(h % 2) * 64 + 64, h, :], pbufB[s][h * 64:(h + 1) * 64, q0:q0 + QS],
                              reads=["pb_" + s], writes=["qB"])
                    d0 = q0 - 128
                    dlo, dhi = max(0, d0), min(L, q0 + QS + 128)
                    for (buf, key, row0) in ((kD, "kD", 1024), (vD, "vD", 1152)):
                        if dlo > d0:
                            P.op("pool", lambda e, buf=buf, n=dlo - d0: e.memset(buf[:, 0:n], 0.0), writes=[key])
                        if dhi < d0 + QS + 256:
                            P.op("pool", lambda e, buf=buf, a=dhi - d0: e.memset(buf[:, a:QS + 256], 0.0), writes=[key])
                        P.dma("sp", "l" + key, buf[:, dlo - d0:dhi - d0], pbufB[s][row0:row0 + 128, dlo:dhi], reads=["pb_" + s], writes=[key])
                    for h in range(4):
                        kv, g = h // 2, h % 2
                        P.dma("act", "lqD", qD[kv * 64:(kv + 1) * 64, h, :], pbufB[s][768 + h * 64:768 + (h + 1) * 64, q0:q0 + QS],
                              reads=["pb_" + s], writes=["qD"])

                    def vrange(glob0, step):
                        ok = [0 <= glob0 + step * i < L for i in range(128)]
                        if not any(ok):
                            return None
                        a = ok.index(True)
                        b = 128 - ok[::-1].index(True)
                        assert all(ok[a:b]) and a in (0, 64) and b in (64, 128), (a, b)
                        return (a, b)

                    def build_v(src_ap, dst3, nheads, valid, vkey, skey):
                        pi = 6 + cnt["tr"] % 2
                        cnt["tr"] += 1
                        pv = psum[pi][:, 0:64].bitcast(BF16)
                        P.op("pe", lambda e, pv=pv, src_ap=src_ap: e.transpose(pv, src_ap, ident[:]),
                             reads=[skey, "ident"], writes=["ps%d" % pi])
                        P.op("act", lambda e, pv=pv, dst3=dst3: e.copy(out=dst3[:, :, 0:64], in_=pv.rearrange("p (h d) -> p h d", h=2)),
                             reads=["ps%d" % pi], writes=[vkey])

                    P.op("pool", lambda e: e.memset(vaB.rearrange("p t (h d) -> p t h d", h=4)[:, :, :, 64:65], 1.0), writes=["vaB"])
                    P.op("pool", lambda e: e.memset(vaD.rearrange("p t (h d) -> p t h d", h=2)[:, :, :, 64:65], 1.0), writes=["vaD"])
                    validB, validD = {}, {}
                    for (dil, ntile) in PATS:
                        for r in range(dil):
                            for m in range(ntile):
                                sw = r + HALO + dil * (128 * m - 64)
                                vr = vrange(w0 + sw, dil)
                                t = tbase[(dil, r)] + m
                                validB[t] = vr
                                if vr is None:
                                    continue
                                for c in range(2):
                                    dst3 = vaB[:, t, :].rearrange("p (h d) -> p h d", h=4)[:, 2 * c:2 * c + 2, :]
                                    build_v(vB[:, c, sw:sw + 127 * dil + 1:dil], dst3, 2, vr, "vaB", "vB")
                                if vr != (0, 128):
                                    a, b = (0, 64) if vr[0] == 64 else (64, 128)
                                    P.op("pool", lambda e, t=t, a=a, b=b: e.memset(vaB[a:b, t, :], 0.0), reads=["vaB"], writes=["vaB"])
                    for m in range(QS // 128 + 2):
                        vr = vrange(d0 + 128 * m, 1)
                        validD[m] = vr
                        if vr is None:
                            continue
                        assert vr == (0, 128)
                        dst3 = vaD[:, m, :].rearrange("p (h d) -> p h d", h=2)
                        build_v(vD[:, 128 * m:128 * m + 128], dst3, 2, vr, "vaD", "vD")

                    def run_pattern(pname, dil, first):
                        blocks = []
                        nqb = QS // dil // 128
                        for r in range(dil):
                            for jb in range(nqb):
                                blocks.append((r, jb))
                        pend = None
                        for bi_, blk_ in enumerate(blocks + [None]):
                            cur = None
                            if blk_ is not None:
                                r, jb = blk_
                                qsl = slice(r + dil * 128 * jb, r + dil * 128 * jb + 127 * dil + 1, dil)
                                tiles = []
                                nr = 3 if pname == "D" else 2
                                for rr in range(nr):
                                    if pname == "D":
                                        t = jb + rr
                                        if validD[t] is None:
                                            continue
                                        tiles.append((rr, t, rr))
                                    else:
                                        t = tbase[(dil, r)] + jb + rr
                                        if validB[t] is None:
                                            continue
                                        tiles.append((rr, t, 3 + 2 * [1, 4, 16].index(dil) + rr))
                                banks = [(bi_ % 2) * 3 + x for x in range(3)]
                                plist = []
                                for ti, (rr, t, widx) in enumerate(tiles):
                                    bk = banks[ti]
                                    for h in range(4):
                                        if pname == "D":
                                            kv, g = h // 2, h % 2
                                            lhs = kD[:, 128 * t:128 * t + 128]
                                            rhs = qD[:, h, qsl]
                                            rk = ["kD", "qD"]
                                        else:
                                            sw = r + HALO + dil * (128 * (jb + rr) - 64)
                                            lhs = kB[:, h // 2, sw:sw + 127 * dil + 1:dil]
                                            rhs = qB[:, h, qsl]
                                            rk = ["kB", "qB"]
                                        P.op("pe", lambda e, bk=bk, h=h, lhs=lhs, rhs=rhs: e.matmul(
                                            psum[bk][:, h * 128:(h + 1) * 128], lhsT=lhs, rhs=rhs, start=True, stop=True),
                                            reads=rk, writes=["ps%d" % bk])
                                    ei = cnt["es"] % 2
                                    pi_ = cnt["es"] % 6
                                    cnt["es"] += 1
                                    P.op("act", lambda e, bk=bk, ei=ei: e.activation(out=es[ei][:], in_=psum[bk][:, :], func=AF.Exp, scale=0.125),
                                         reads=["ps%d" % bk], writes=["es%d" % ei])
                                    P.op("dve", lambda e, ei=ei, pi_=pi_, widx=widx: e.tensor_tensor(out=pts[pi_][:], in0=es[ei][:], in1=wt[:, widx, :], op=ALU.mult),
                                         reads=["es%d" % ei, "wt"], writes=["pt%d" % pi_])
                                    plist.append((pi_, t))
                                cur = (plist, qsl, bi_)
                            if pend is not None:
                                plist, pqsl, pb_ = pend
                                ob = 6 + pb_ % 2
                                for h in range(4):
                                    for ti, (pi_, t) in enumerate(plist):
                                        if pname == "D":
                                            lhs = vaD[:, t, (h // 2) * 66:(h // 2) * 66 + 65]
                                            vk = "vaD"
                                        else:
                                            lhs = vaB[:, t, h * 66:h * 66 + 65]
                                            vk = "vaB"
                                        P.op("pe", lambda e, ob=ob, h=h, lhs=lhs, pi_=pi_, ti=ti, n=len(plist): e.matmul(
                                            psum[ob][0:65, h * 128:(h + 1) * 128], lhsT=lhs, rhs=pts[pi_][:, h * 128:(h + 1) * 128],
                                            start=(ti == 0), stop=(ti == n - 1)), reads=[vk, "pt%d" % pi_], writes=["ps%d" % ob])
                                av = acc[0:65, :, pqsl]
                                pvv = psum[ob][0:65, :].rearrange("p (h q) -> p h q", h=4)
                                if first:
                                    P.op("dve", lambda e, av=av, pvv=pvv: e.tensor_copy(out=av, in_=pvv), reads=["ps%d" % ob], writes=["acc"])
                                else:
                                    P.op("dve", lambda e, av=av, pvv=pvv: e.tensor_tensor(out=av, in0=pvv, in1=av, op=ALU.add),
                                         reads=["ps%d" % ob, "acc"], writes=["acc"])
                            pend = cur

                    def normalize(pname, row0):
                        for h in range(4):
                            for cb in range(QS // 512):
                                cs = slice(cb * 512, (cb + 1) * 512)
                                pi = 6 + cnt["ost"] % 2
                                oi = cnt["ost"] % 2
                                cnt["ost"] += 1
                                P.op("pe", lambda e, pi=pi, h=h, cs=cs: e.matmul(psum[pi][0:64, :], lhsT=sel[0:65, :], rhs=acc[0:65, h, cs],
                                                                                 start=True, stop=True), reads=["acc", "sel"], writes=["ps%d" % pi])
                                if pname == "D":
                                    P.op("dve", lambda e, pi=pi, h=h: e.tensor_scalar(out=rec[0:64, :], in0=psum[pi][0:64, :],
                                                                                      scalar1=sinke[:, l * 4 + h:l * 4 + h + 1], scalar2=None, op0=ALU.add),
                                         reads=["ps%d" % pi, "sinke"], writes=["es0"])
                                    P.op("dve", lambda e: e.reciprocal(out=rec[0:64, :], in_=rec[0:64, :]), reads=["es0"], writes=["es0"])
                                else:
                                    P.op("dve", lambda e, pi=pi: e.reciprocal(out=rec[0:64, :], in_=psum[pi][0:64, :]), reads=["ps%d" % pi], writes=["es0"])
                                P.op("pool", lambda e, oi=oi, h=h, cs=cs: e.tensor_tensor(out=ost[oi][0:64, :], in0=acc[0:64, h, cs], in1=rec[0:64, :], op=ALU.mult),
                                     reads=["acc", "es0"], writes=["pt%d" % oi])
                                P.dma("sp", "so%d" % oi, obuf[s][row0 + 64 * h:row0 + 64 * h + 64, q0 + cb * 512:q0 + (cb + 1) * 512], ost[oi][0:64, :],
                                      reads=["pt%d" % oi], writes=["ob_" + s])

                    run_pattern("D", 1, True)
                    normalize("D", 768)
                    run_pattern("B", 1, True)
                    run_pattern("B", 4, False)
                    run_pattern("B", 16, False)
                    normalize("B", 256)


        NB = 512

        def s5(l):
            import math
            P.barrier()
            apos[0] = 0
            Bt = carve([128, 16, 128], F32)
            Ct = carve([128, 32, 128], F32)
            par = carve([128, depth * 48], F32)
            E = carve([128, 32, NB], F32)
            rdec = carve([128, NB], F32)
            ub = [carve([128, 2, NB], F32) for _ in range(2)]
            yb = [carve([128, 2, NB], F32) for _ in range(2)]
            W = {k: carve([128, NB], F32) for k in ("t1", "t2", "t3", "t4", "xre", "xim", "sre", "sim", "Sre", "Sim")}
            sm = carve([128, 16, 16], F32)
            ini = carve([128, 16, 2], F32)
            wg = carve([128, 2, 512], F32)
            dsk = carve([128, depth * 2], F32)
            gl = [carve([128, NB], F32) for _ in range(4)]
            ost = [carve([128, NB], BF16) for _ in range(2)]
            onesN = carve([128, NB], F32)
            P.op("pool", lambda e: e.memset(onesN[:], 1.0), writes=["onesN"])
            P.dma("sp", "c0", Bt[:], s5B_d[l].rearrange("p (a b) -> p a b", a=16), writes=["Bt"])
            P.dma("sp", "c0", Ct[:], s5C_d[l].rearrange("p (a b) -> p a b", a=32), writes=["Ct"])
            P.dma("sp", "c0", par[:], s5par_d[:, :], writes=["par"])
            P.dma("sp", "c0", wg[:], wglu_d[l].rearrange("(k p) c -> p k c", p=128), writes=["wg"])
            P.dma("sp", "c0", dsk[:], s5d_d[:, :], writes=["dsk"])
            pv = par[:, l * 48:(l + 1) * 48].rearrange("p (a k) -> p a k", k=3)
            lr, li, ls = pv[:, :, 0], pv[:, :, 1], pv[:, :, 2]
            Q = lambda i: sm[:, i, :]
            SM = ["sm"]

            def vv(out, a, b, op):
                P.op("dve", lambda e: e.tensor_tensor(out=out, in0=a, in1=b, op=op), reads=SM + ["par"], writes=SM)

            def vs(out, a, s1, op0, s2=None, op1=None):
                if op1 is None:
                    P.op("dve", lambda e: e.tensor_single_scalar(out=out, in_=a, scalar=s1, op=op0), reads=SM + ["par"], writes=SM)
                else:
                    P.op("dve", lambda e: e.tensor_scalar(out=out, in0=a, scalar1=s1, scalar2=s2, op0=op0, op1=op1), reads=SM + ["par"], writes=SM)

            def act(out, a, func, bias=0.0, scale=1.0):
                P.op("act", lambda e: e.activation(out=out, in_=a, func=func, bias=bias, scale=scale), reads=SM + ["par"], writes=SM)
            STEP, MAG, ANG, SIN, COS, ABR, ABI, INV, KR, KI, T0, T1 = range(12)
            act(Q(STEP), ls, AF.Exp)
            vv(Q(T0), lr, Q(STEP), ALU.mult)
            act(Q(MAG), Q(T0), AF.Exp)
            vv(Q(ANG), li, Q(STEP), ALU.mult)

            def sin_of(dst, shift):
                ki = sm[:, 15, :].bitcast(mybir.dt.int32)
                vs(Q(T0), Q(ANG), shift + 2 * math.pi, ALU.add)
                vs(Q(T1), Q(T0), 1.0 / (2 * math.pi), ALU.mult)
                P.op("dve", lambda e: e.tensor_copy(out=ki, in_=Q(T1)), reads=SM, writes=SM)
                P.op("dve", lambda e: e.tensor_copy(out=Q(T1), in_=ki), reads=SM, writes=SM)
                P.op("dve", lambda e: e.scalar_tensor_tensor(out=Q(T0), in0=Q(T1), scalar=-2 * math.pi, in1=Q(T0), op0=ALU.mult, op1=ALU.add),
                     reads=SM, writes=SM)
                vs(Q(T1), Q(T0), -math.pi, ALU.add, 0.0, ALU.max)
                vs(Q(T1), Q(T1), 1e30, ALU.mult, 1.0, ALU.min)
                P.op("dve", lambda e: e.scalar_tensor_tensor(out=Q(T0), in0=Q(T1), scalar=-2 * math.pi, in1=Q(T0), op0=ALU.mult, op1=ALU.add),
                     reads=SM, writes=SM)
                vs(Q(T0), Q(T0), -math.pi, ALU.max, math.pi, ALU.min)
                act(dst, Q(T0), AF.Sin)
            sin_of(Q(SIN), 0.0)
            sin_of(Q(COS), 0.5 * math.pi)
            vv(Q(ABR), Q(MAG), Q(COS), ALU.mult)
            vv(Q(ABI), Q(MAG), Q(SIN), ALU.mult)
            vv(Q(T0), lr, lr, ALU.mult)
            vv(Q(T1), li, li, ALU.mult)
            vv(Q(T0), Q(T0), Q(T1), ALU.add)
            P.op("dve", lambda e: e.reciprocal(out=Q(INV), in_=Q(T0)), reads=SM, writes=SM)
            vs(Q(T0), Q(ABR), -1.0, ALU.add)
            vv(Q(T1), Q(T0), lr, ALU.mult)
            vv(Q(KR), Q(ABI), li, ALU.mult)
            vv(Q(KR), Q(KR), Q(T1), ALU.add)
            vv(Q(KR), Q(KR), Q(INV), ALU.mult)
            vv(Q(T1), Q(T0), li, ALU.mult)
            vv(Q(KI), Q(ABI), lr, ALU.mult)
            vv(Q(KI), Q(KI), Q(T1), ALU.subtract)
            vv(Q(KI), Q(KI), Q(INV), ALU.mult)
            for dj in range(16):
                cr, ci = Ct[:, 2 * dj, :], Ct[:, 2 * dj + 1, :]
                kr, ki = sm[:, KR, dj:dj + 1], sm[:, KI, dj:dj + 1]
                t1v, t2v = W["t1"][:, 0:128], W["t2"][:, 0:128]
                P.op("dve", lambda e, ci=ci, ki=ki, t1v=t1v: e.tensor_scalar(out=t1v, in0=ci, scalar1=ki, scalar2=None, op0=ALU.mult),
                     reads=SM + ["Ct"], writes=["t1"])
                P.op("dve", lambda e, cr=cr, ki=ki, t2v=t2v: e.tensor_scalar(out=t2v, in0=cr, scalar1=ki, scalar2=-1.0, op0=ALU.mult, op1=ALU.mult),
                     reads=SM + ["Ct"], writes=["t2"])
                P.op("dve", lambda e, cr=cr, kr=kr, t1v=t1v: e.scalar_tensor_tensor(out=cr, in0=cr, scalar=kr, in1=t1v, op0=ALU.mult, op1=ALU.subtract),
                     reads=SM + ["Ct", "t1"], writes=["Ct"])
                P.op("dve", lambda e, ci=ci, kr=kr, t2v=t2v: e.scalar_tensor_tensor(out=ci, in0=ci, scalar=kr, in1=t2v, op0=ALU.mult, op1=ALU.subtract),
                     reads=SM + ["Ct", "t2"], writes=["Ct"])
                P.op("dve", lambda e, ci=ci: e.tensor_single_scalar(out=ci, in_=ci, scalar=-1.0, op=ALU.mult), reads=["Ct"], writes=["Ct"])
            for dj in range(16):
                ec, es_ = E[:, 2 * dj, :], E[:, 2 * dj + 1, :]
                c1, s1 = sm[:, COS, dj:dj + 1], sm[:, SIN, dj:dj + 1]
                P.op("pool", lambda e, ec=ec: e.memset(ec[:, 0:1], 1.0), writes=["E"])
                P.op("pool", lambda e, es_=es_: e.memset(es_[:, 0:1], 0.0), writes=["E"])
                P.op("dve", lambda e, ec=ec, c1=c1: e.tensor_copy(out=ec[:, 1:2], in_=c1), reads=SM + ["E"], writes=["E"])
                P.op("dve", lambda e, es_=es_, s1=s1: e.tensor_copy(out=es_[:, 1:2], in_=s1), reads=SM + ["E"], writes=["E"])
                m = 2
                while m < NB:
                    a_c, a_s = ec[:, m - 1:m], es_[:, m - 1:m]
                    sc_c, sc_s = ini[:, dj, 0:1], ini[:, dj, 1:2]
                    tt = sm[:, T0, dj:dj + 1]
                    P.op("dve", lambda e, a_s=a_s, s1=s1, tt=tt: e.tensor_tensor(out=tt, in0=a_s, in1=s1, op=ALU.mult), reads=SM + ["E"], writes=SM)
                    P.op("dve", lambda e, a_c=a_c, c1=c1, tt=tt, sc_c=sc_c: e.scalar_tensor_tensor(out=sc_c, in0=a_c, scalar=c1, in1=tt, op0=ALU.mult, op1=ALU.subtract),
                         reads=SM + ["E"], writes=["ini"])
                    P.op("dve", lambda e, a_c=a_c, s1=s1, tt=tt: e.tensor_tensor(out=tt, in0=a_c, in1=s1, op=ALU.mult), reads=SM + ["E"], writes=SM)
                    P.op("dve", lambda e, a_s=a_s, c1=c1, tt=tt, sc_s=sc_s: e.scalar_tensor_tensor(out=sc_s, in0=a_s, scalar=c1, in1=tt, op0=ALU.mult, op1=ALU.add),
                         reads=SM + ["E"], writes=["ini"])
                    lo_c, lo_s = ec[:, 0:m], es_[:, 0:m]
                    hi_c, hi_s = ec[:, m:2 * m], es_[:, m:2 * m]
                    tv = W["t1"][:, 0:m]
                    P.op("dve", lambda e, lo_s=lo_s, sc_s=sc_s, tv=tv: e.tensor_scalar(out=tv, in0=lo_s, scalar1=sc_s, scalar2=None, op0=ALU.mult),
                         reads=["E", "ini"], writes=["t1"])
                    P.op("dve", lambda e, lo_c=lo_c, sc_c=sc_c, tv=tv, hi_c=hi_c: e.scalar_tensor_tensor(out=hi_c, in0=lo_c, scalar=sc_c, in1=tv, op0=ALU.mult, op1=ALU.subtract),
                         reads=["E", "ini", "t1"], writes=["E"])
                    P.op("dve", lambda e, lo_c=lo_c, sc_s=sc_s, tv=tv: e.tensor_scalar(out=tv, in0=lo_c, scalar1=sc_s, scalar2=None, op0=ALU.mult),
                         reads=["E", "ini"], writes=["t1"])
                    P.op("dve", lambda e, lo_s=lo_s, sc_c=sc_c, tv=tv, hi_s=hi_s: e.scalar_tensor_tensor(out=hi_s, in0=lo_s, scalar=sc_c, in1=tv, op0=ALU.mult, op1=ALU.add),
                         reads=["E", "ini", "t1"], writes=["E"])
                    m *= 2

            cnt = {"b": 0, "x": 0, "o": 0}
            for s, L in seqs:
                nblk = L // NB
                for d in range(2):
                    P.op("pool", lambda e: e.memset(ini[:], 0.0), reads=["E"], writes=["ini"])
                    order = range(nblk) if d == 0 else range(nblk - 1, -1, -1)
                    R = (lambda ap: ap) if d == 0 else (lambda ap: ap[:, ::-1])
                    for bk in order:
                        t0 = bk * NB
                        bi = cnt["b"] % 2
                        cnt["b"] += 1
                        uk, yk = "ub%d" % bi, "yb%d" % bi
                        P.dma("sp", "su%d" % bi, ub[bi][:], pbufA[s][1024:1280, t0:t0 + NB].rearrange("(c p) n -> p c n", p=128),
                              reads=["pa_" + s], writes=[uk])
                        if d == 1:
                            P.dma("act", "sy%d" % bi, yb[bi][:], ybuf[s][:, t0:t0 + NB].rearrange("(c p) n -> p c n", p=128),
                                  reads=["yb_" + s], writes=[yk])
                        for ct in range(2):
                            ybank = 4 + ct
                            for jj in range(4):
                                j = ct * 4 + jj
                                dj = d * 8 + j
                                pa, pb = (cnt["x"] % 2) * 2, (cnt["x"] % 2) * 2 + 1
                                cnt["x"] += 1
                                for (pp, c) in ((pa, 0), (pb, 1)):
                                    P.op("pe", lambda e, pp=pp, c=c, j=j, bi=bi, ct=ct: e.matmul(psum[pp][:, :NB], lhsT=Bt[:, 2 * j + c, :], rhs=ub[bi][:, ct, :],
                                                                                                 start=True, stop=True), reads=["Bt", uk], writes=["ps%d" % pp])
                                ec, es_ = E[:, 2 * dj, :], E[:, 2 * dj + 1, :]
                                Xr, Xi = R(psum[pa][:, :NB]), R(psum[pb][:, :NB])
                                P.op("act", lambda e, dj=dj: e.mul(out=rdec[:], in_=onesN[:], mul=sm[:, MAG, dj:dj + 1]), reads=SM + ["onesN"], writes=["rdec"])
                                tt = lambda eng, o, a, b, op, rk, wk: P.op(eng, lambda e: e.tensor_tensor(out=W[o][:] if isinstance(o, str) else o, in0=a, in1=b, op=op),
                                                                           reads=rk, writes=wk)
                                tt("dve", "t1", Xr, ec, ALU.mult, ["ps%d" % pa, "E"], ["t1"])
                                tt("dve", "t2", Xi, es_, ALU.mult, ["ps%d" % pb, "E"], ["t2"])
                                tt("pool", "xre", W["t1"][:], W["t2"][:], ALU.add, ["t1", "t2"], ["xre"])
                                tt("dve", "t3", Xi, ec, ALU.mult, ["ps%d" % pb, "E"], ["t3"])
                                tt("dve", "t4", Xr, es_, ALU.mult, ["ps%d" % pa, "E"], ["t4"])
                                tt("pool", "xim", W["t3"][:], W["t4"][:], ALU.subtract, ["t3", "t4"], ["xim"])
                                P.op("dve", lambda e, dj=dj: e.tensor_tensor_scan(out=W["sre"][:], data0=rdec[:], data1=W["xre"][:], initial=ini[:, dj, 0:1],
                                                                                    op0=ALU.mult, op1=ALU.add), reads=["rdec", "xre", "ini"], writes=["sre"])
                                P.op("dve", lambda e, dj=dj: e.tensor_tensor_scan(out=W["sim"][:], data0=rdec[:], data1=W["xim"][:], initial=ini[:, dj, 1:2],
                                                                                    op0=ALU.mult, op1=ALU.add), reads=["rdec", "xim", "ini"], writes=["sim"])
                                tt("pool", "t1", ec, W["sre"][:], ALU.mult, ["E", "sre"], ["t1"])
                                tt("pool", "t2", es_, W["sim"][:], ALU.mult, ["E", "sim"], ["t2"])
                                tt("dve", R(W["Sre"][:]), W["t1"][:], W["t2"][:], ALU.subtract, ["t1", "t2"], ["Sre"])
                                tt("pool", "t3", es_, W["sre"][:], ALU.mult, ["E", "sre"], ["t3"])
                                tt("pool", "t4", ec, W["sim"][:], ALU.mult, ["E", "sim"], ["t4"])
                                tt("dve", R(W["Sim"][:]), W["t3"][:], W["t4"][:], ALU.add, ["t3", "t4"], ["Sim"])
                                ce = NB - 1 if d == 0 else 0
                                sr_, si_ = W["Sre"][:, ce:ce + 1], W["Sim"][:, ce:ce + 1]
                                c1, s1 = sm[:, COS, dj:dj + 1], sm[:, SIN, dj:dj + 1]
                                tq = sm[:, T1, dj:dj + 1]
                                P.op("dve", lambda e, si_=si_, s1=s1, tq=tq: e.tensor_tensor(out=tq, in0=si_, in1=s1, op=ALU.mult), reads=SM + ["Sim"], writes=SM)
                                P.op("dve", lambda e, sr_=sr_, c1=c1, tq=tq, dj=dj: e.scalar_tensor_tensor(out=ini[:, dj, 0:1], in0=sr_, scalar=c1, in1=tq, op0=ALU.mult, op1=ALU.subtract),
                                     reads=SM + ["Sre"], writes=["ini"])
                                P.op("dve", lambda e, sr_=sr_, s1=s1, tq=tq: e.tensor_tensor(out=tq, in0=sr_, in1=s1, op=ALU.mult), reads=SM + ["Sre"], writes=SM)
                                P.op("dve", lambda e, si_=si_, c1=c1, tq=tq, dj=dj: e.scalar_tensor_tensor(out=ini[:, dj, 1:2], in0=si_, scalar=c1, in1=tq, op0=ALU.mult, op1=ALU.add),
                                     reads=SM + ["Sim"], writes=["ini"])
                                P.op("pe", lambda e, ybank=ybank, dj=dj, jj=jj: e.matmul(psum[ybank][:, :NB], lhsT=Ct[:, 2 * dj, :], rhs=W["Sre"][:], start=(jj == 0), stop=False),
                                     reads=["Ct", "Sre"], writes=["ps%d" % ybank])
                                P.op("pe", lambda e, ybank=ybank, dj=dj, jj=jj: e.matmul(psum[ybank][:, :NB], lhsT=Ct[:, 2 * dj + 1, :], rhs=W["Sim"][:], start=False, stop=(jj == 3)),
                                     reads=["Ct", "Sim"], writes=["ps%d" % ybank])
                            if d == 0:
                                P.op("dve", lambda e, ybank=ybank, bi=bi, ct=ct: e.scalar_tensor_tensor(out=yb[bi][:, ct, :], in0=ub[bi][:, ct, :], scalar=dsk[:, l * 2 + ct:l * 2 + ct + 1],
                                                                                                      in1=psum[ybank][:, :NB], op0=ALU.mult, op1=ALU.add),
                                     reads=[uk, "dsk", "ps%d" % ybank], writes=[yk])
                            else:
                                P.op("dve", lambda e, ybank=ybank, bi=bi, ct=ct: e.tensor_tensor(out=yb[bi][:, ct, :], in0=yb[bi][:, ct, :], in1=psum[ybank][:, :NB], op=ALU.add),
                                     reads=[yk, "ps%d" % ybank], writes=[yk])
                        if d == 0:
                            P.dma("pool", "sys%d" % bi, ybuf[s][:, t0:t0 + NB].rearrange("(c p) n -> p c n", p=128), yb[bi][:], reads=[yk], writes=["yb_" + s])
                        else:
                            for ct in range(2):
                                yv = yb[bi][:, ct, :]
                                P.op("pool", lambda e, yv=yv: e.tensor_tensor(out=gl[0][:], in0=yv, in1=yv, op=ALU.mult), reads=[yk], writes=["gl0"])
                                P.op("pool", lambda e: e.tensor_scalar(out=gl[0][:], in0=gl[0][:], scalar1=0.044715, scalar2=1.0, op0=ALU.mult, op1=ALU.add), reads=["gl0"], writes=["gl0"])
                                P.op("pool", lambda e, yv=yv: e.tensor_tensor(out=gl[0][:], in0=gl[0][:], in1=yv, op=ALU.mult), reads=[yk, "gl0"], writes=["gl0"])
                                P.op("act", lambda e: e.activation(out=gl[0][:], in_=gl[0][:], func=AF.Sigmoid, scale=2.0 * math.sqrt(2.0 / math.pi)), reads=["gl0"], writes=["gl0"])
                                P.op("pool", lambda e, yv=yv, ct=ct: e.tensor_tensor(out=gl[1 + ct][:], in0=gl[0][:], in1=yv, op=ALU.mult), reads=[yk, "gl0"], writes=["gl%d" % (1 + ct)])
                            for oc in range(2):
                                pa, pb = 6, 7
                                for (pp, col0) in ((pa, oc * 128), (pb, 256 + oc * 128)):
                                    for k in range(2):
                                        P.op("pe", lambda e, pp=pp, col0=col0, k=k: e.matmul(psum[pp][:, :NB], lhsT=wg[:, k, col0:col0 + 128], rhs=gl[1 + k][:],
                                                                                             start=(k == 0), stop=(k == 1)), reads=["wg", "gl%d" % (1 + k)], writes=["ps%d" % pp])
                                oi = cnt["o"] % 2
                                cnt["o"] += 1
                                P.op("act", lambda e: e.activation(out=gl[3][:], in_=psum[7][:, :NB], func=AF.Sigmoid), reads=["ps7"], writes=["gl3"])
                                P.op("dve", lambda e, oi=oi: e.tensor_tensor(out=ost[oi][:], in0=psum[6][:, :NB], in1=gl[3][:], op=ALU.mult), reads=["ps6", "gl3"], writes=["s5o%d" % oi])
                                P.dma("sp", "s5s%d" % oi, obuf[s][512 + oc * 128:512 + (oc + 1) * 128, t0:t0 + NB], ost[oi][:], reads=["s5o%d" % oi], writes=["ob_" + s])


        def gdn(l):
            P.barrier()
            apos[0] = 0
            GB = 512
            xin_t = carve([128, 6, GB + 4], F32)
            cv = carve([128, 6, GB], F32)
            zt = carve([128, 2, GB], F32)
            sqb = carve([128, GB], F32)
            rsb = carve([128, GB], F32)
            qpad = carve([128, 4, GB], F32)
            kpad = carve([128, 4, GB], F32)
            MK = carve([128, 8, 128], F32)
            negones = carve([128, 128], F32)
            cw = carve([128, depth * 30], F32)
            gp = carve([128, depth * 16], F32)
            gn = carve([128, depth * 256], F32)
            ab = carve([128, 4, 16], F32)
            gt = {k: carve([128, 4, 8], F32) for k in ("g", "beta", "sp")}
            tq = {k: carve([128, 8], F32) for k in ("gc", "gtot", "eg", "ekd", "bge", "gl0", "gl1")}
            ktok = carve([128, 2, 128], F32)
            vtok = carve([128, 2, 128], F32)
            Wb = {k: carve([128, 128], F32) for k in ("Ug", "Dcs", "DT", "tm1", "tm2", "A0", "A1", "B0", "B1", "T0", "T1", "qkT", "WT",
                                                       "Kbg0", "Kbg1", "kd0", "kd1")}
            Sb = {k: carve([128, 64], F32) for k in ("Vb", "U", "vnA", "vnB", "tt")}
            otile = carve([128, 4, 64], F32)
            ofw = carve([128, 4, 64], F32)
            Sst = carve([128, 4, 64], F32)
            fin = {k: carve([128, 4, 64], F32) for k in ("o2", "on")}
            ssq = carve([128, 8], F32)
            oT = [carve([128, 128], BF16) for _ in range(2)]
            MLI, MLS, MUI, MUS, MBL, IN0, IN1, IDF = range(8)
            P.dma("sp", "c0", MK[:], gmask_d[:, :].rearrange("p (a b) -> p a b", a=8), writes=["MK"])
            P.dma("sp", "c0", cw[:], gconv_d[:, :], writes=["cw"])
            P.dma("sp", "c0", gp[:], gpar_d[:, :], writes=["gp"])
            P.dma("sp", "c0", gn[:], gnorm_d[:, :], writes=["gn"])
            P.op("pool", lambda e: e.memset(negones[:], -1.0), writes=["negones"])
            P.op("pool", lambda e: e.memset(qpad[:], 0.0), writes=["qpad"])
            P.op("pool", lambda e: e.memset(kpad[:], 0.0), writes=["kpad"])
            for k in ("Kbg0", "Kbg1", "kd0", "kd1"):
                P.op("pool", lambda e, k=k: e.memset(Wb[k][:], 0.0), writes=[k])
            P.op("pool", lambda e: e.memset(Sb["vnA"][:], 0.0), writes=["vnA"])
            P.op("pool", lambda e: e.memset(Sb["vnB"][:], 0.0), writes=["vnB"])
            alog = gp[:, l * 16:l * 16 + 8]
            dtb = gp[:, l * 16 + 8:l * 16 + 16]
            P.op("act", lambda e: e.activation(out=alog, in_=alog, func=AF.Exp), reads=["gp"], writes=["gp"])
            cwl = cw[:, l * 30:(l + 1) * 30].rearrange("p (c j) -> p c j", j=5)
            gnl = gn[:, l * 256:(l + 1) * 256].rearrange("p (h d) -> p h d", h=4)

            dbgc = {"n": 0}

            def dump(ap, key, w, cond):
                if not (test_mix and not lean and cond):
                    return
                i = dbgc["n"]
                dbgc["n"] += 1
                P.dma("sp", "dbg", dbg_d[:, i * 128:i * 128 + w], ap, reads=[key], writes=["dbg"])

            def mm(out, lhsT, rhs, reads, pi, start=True, stop=True):
                P.op("pe", lambda e: e.matmul(out, lhsT=lhsT, rhs=rhs, start=start, stop=stop), reads=reads, writes=["ps%d" % pi])

            for s, L in seqs:
                for d in range(2):
                    rev = d == 1
                    Mc_i, Mc_s, McT_i = (MLI, MLS, MUI) if not rev else (MUI, MUS, MLI)
                    Ucum = MUI if not rev else MLI
                    P.op("pool", lambda e: e.memset(Sst[:], 0.0), writes=["Sst"])
                    blocks = list(range(0, L, GB))
                    if rev:
                        blocks = blocks[::-1]
                    for t0 in blocks:
                        lo, hi = max(0, t0 - 2), min(L, t0 + GB + 2)
                        if lo > t0 - 2:
                            P.op("pool", lambda e: e.memset(xin_t[:, :, 0:2], 0.0), writes=["xin"])
                        if hi < t0 + GB + 2:
                            P.op("pool", lambda e: e.memset(xin_t[:, :, GB + 2:GB + 4], 0.0), writes=["xin"])
                        P.dma("sp", "gx", xin_t[:, :, lo - (t0 - 2):hi - (t0 - 2)], pbufA[s][0:768, lo:hi].rearrange("(c p) n -> p c n", p=128),
                              reads=["pa_" + s], writes=["xin"])
                        P.dma("act", "gz", zt[:], pbufA[s][768:1024, t0:t0 + GB].rearrange("(c p) n -> p c n", p=128), reads=["pa_" + s], writes=["zt"])
                        P.dma("sp", "gab", ab[:], abuf[s][t0:t0 + GB, :].rearrange("(n p) c -> p n c", p=128), reads=["ab_" + s], writes=["ab"])
                        for c in range(6):
                            P.op("dve", lambda e, c=c: e.tensor_scalar(out=cv[:, c, :], in0=xin_t[:, c, 0:GB], scalar1=cwl[:, c, 0:1], scalar2=None, op0=ALU.mult),
                                 reads=["xin", "cw"], writes=["cv%d" % c])
                            for j in range(1, 5):
                                P.op("dve", lambda e, c=c, j=j: e.scalar_tensor_tensor(out=cv[:, c, :], in0=xin_t[:, c, j:j + GB], scalar=cwl[:, c, j:j + 1], in1=cv[:, c, :],
                                                                                        op0=ALU.mult, op1=ALU.add), reads=["xin", "cw", "cv%d" % c], writes=["cv%d" % c])
                            P.op("act", lambda e, c=c: e.activation(out=cv[:, c, :], in_=cv[:, c, :], func=AF.Silu), reads=["cv%d" % c], writes=["cv%d" % c])
                        for c in range(2):
                            P.op("act", lambda e, c=c: e.activation(out=zt[:, c, :], in_=zt[:, c, :], func=AF.Silu), reads=["zt"], writes=["zt"])
                        for c in range(4):
                            P.op("pool", lambda e, c=c: e.tensor_tensor(out=sqb[:], in0=cv[:, c, :], in1=cv[:, c, :], op=ALU.mult), reads=["cv%d" % c], writes=["sqb"])
                            pi = next_ps()
                            mm(psum[pi][:, :GB], MK[:, MBL, :], sqb[:], ["MK", "sqb"], pi)
                            P.op("act", lambda e, pi=pi: e.activation(out=rsb[:], in_=psum[pi][:, :GB], func=AF.Sqrt, bias=epsb[:, 0:1]), reads=["ps%d" % pi, "epsb"], writes=["rsb"])
                            P.op("dve", lambda e: e.reciprocal(out=rsb[:], in_=rsb[:]), reads=["rsb"], writes=["rsb"])
                            dst = qpad if c < 2 else kpad
                            dk = "qpad" if c < 2 else "kpad"
                            sc = 0.125 if c < 2 else 1.0
                            for hh in range(2):
                                h = (c % 2) * 2 + hh
                                rs_ = slice(hh * 64, hh * 64 + 64)
                                P.op("dve", lambda e, c=c, h=h, rs_=rs_, dst=dst, sc=sc: e.scalar_tensor_tensor(out=dst[rs_, h, :], in0=cv[rs_, c, :], scalar=sc, in1=rsb[rs_, :],
                                                                                                                  op0=ALU.mult, op1=ALU.mult), reads=["cv%d" % c, "rsb"], writes=[dk])
                            P.op("dve", lambda e, c=c, sc=sc: e.scalar_tensor_tensor(out=cv[:, c, :], in0=cv[:, c, :], scalar=sc, in1=rsb[:], op0=ALU.mult, op1=ALU.mult),
                                 reads=["cv%d" % c, "rsb"], writes=["cv%d" % c])
                        for n in range(4):
                            P.op("dve", lambda e, n=n: e.tensor_tensor(out=gt["sp"][:, n, :], in0=ab[:, n, 0:8], in1=dtb, op=ALU.add), reads=["ab", "gp"], writes=["gsp"])
                        P.op("act", lambda e: e.activation(out=gt["sp"][:], in_=gt["sp"][:], func=AF.Exp), reads=["gsp"], writes=["gsp"])
                        P.op("act", lambda e: e.activation(out=gt["sp"][:], in_=gt["sp"][:], func=AF.Ln, bias=oneb[:, 0:1]), reads=["gsp", "oneb"], writes=["gsp"])
                        for n in range(4):
                            P.op("dve", lambda e, n=n: e.scalar_tensor_tensor(out=gt["g"][:, n, :], in0=gt["sp"][:, n, :], scalar=-1.0, in1=alog, op0=ALU.mult, op1=ALU.mult),
                                 reads=["gsp", "gp"], writes=["gg"])
                        P.op("act", lambda e: e.activation(out=gt["beta"][:], in_=ab[:, :, 8:16], func=AF.Sigmoid), reads=["ab"], writes=["gbeta"])

                        tiles = list(range(4))
                        if rev:
                            tiles = tiles[::-1]
                        for n in tiles:
                            ts_ = slice(n * 128, (n + 1) * 128)
                            tg0 = t0 + n * 128
                            gcol = gt["g"][:, n, :]
                            bcol = gt["beta"][:, n, :]
                            pi = next_ps()
                            mm(psum[pi][:, 0:8], MK[:, Ucum, :], gcol, ["MK", "gg"], pi)
                            P.op("dve", lambda e, pi=pi: e.tensor_copy(out=tq["gc"][:], in_=psum[pi][:, 0:8]), reads=["ps%d" % pi], writes=["gc"])
                            pi = next_ps()
                            mm(psum[pi][:, 0:8], MK[:, MBL, :], gcol, ["MK", "gg"], pi)
                            P.op("dve", lambda e, pi=pi: e.tensor_copy(out=tq["gtot"][:], in_=psum[pi][:, 0:8]), reads=["ps%d" % pi], writes=["gtot"])
                            P.op("act", lambda e: e.activation(out=tq["eg"][:], in_=tq["gc"][:], func=AF.Exp), reads=["gc"], writes=["eg"])
                            P.op("dve", lambda e: e.tensor_tensor(out=tq["ekd"][:], in0=tq["gtot"][:], in1=tq["gc"][:], op=ALU.subtract), reads=["gtot", "gc"], writes=["ekd"])
                            P.op("act", lambda e: e.activation(out=tq["ekd"][:], in_=tq["ekd"][:], func=AF.Exp), reads=["ekd"], writes=["ekd"])
                            P.op("dve", lambda e, bcol=bcol: e.tensor_tensor(out=tq["bge"][:], in0=tq["eg"][:], in1=bcol, op=ALU.mult), reads=["eg", "gbeta"], writes=["bge"])
                            for c_ in range(2):
                                pi = next_ps()
                                mm(psum[pi][:, 0:8], MK[:, IN0 + c_, :], gcol, ["MK", "gg"], pi)
                                P.op("act", lambda e, pi=pi, c_=c_: e.activation(out=tq["gl%d" % c_][:], in_=psum[pi][:, 0:8], func=AF.Exp), reads=["ps%d" % pi], writes=["gl%d" % c_])
                            for c_ in range(2):
                                for (srcc, dstt, dkey) in ((2 + c_, ktok, "ktok"), (4 + c_, vtok, "vtok")):
                                    pi = next_ps()
                                    P.op("pe", lambda e, pi=pi, srcc=srcc, ts_=ts_: e.transpose(psum[pi][:, 0:128], cv[:, srcc, ts_], MK[:, IDF, :]),
                                         reads=["cv%d" % srcc, "MK"], writes=["ps%d" % pi])
                                    P.op("act", lambda e, pi=pi, dstt=dstt, c_=c_: e.copy(out=dstt[:, c_, :], in_=psum[pi][:, 0:128]), reads=["ps%d" % pi], writes=[dkey])
                            for h in range(4):
                                dh = d * 4 + h
                                par_ = h % 2
                                hs = slice(par_ * 64, par_ * 64 + 64)
                                g1, b1 = gcol[:, dh:dh + 1], bcol[:, dh:dh + 1]
                                P.op("dve", lambda e, g1=g1, Ucum=Ucum: e.tensor_scalar(out=Wb["Ug"][:], in0=MK[:, Ucum, :], scalar1=g1, scalar2=None, op0=ALU.mult),
                                     reads=["MK", "gg"], writes=["Ug"])
                                pd = next_ps()
                                mm(psum[pd][:, 0:128], Wb["Ug"][:], ones[:], ["Ug", "ones"], pd, True, False)
                                mm(psum[pd][:, 0:128], negones[:], Wb["Ug"][:], ["Ug", "negones"], pd, False, True)
                                P.op("dve", lambda e, pd=pd: e.tensor_scalar(out=Wb["tm1"][:], in0=psum[pd][:, 0:128], scalar1=0.0, scalar2=None, op0=ALU.min), reads=["ps%d" % pd], writes=["tm1"])
                                P.op("dve", lambda e, pd=pd: e.tensor_scalar(out=Wb["tm2"][:], in0=psum[pd][:, 0:128], scalar1=-1.0, scalar2=0.0, op0=ALU.mult, op1=ALU.min),
                                     reads=["ps%d" % pd], writes=["tm2"])
                                P.op("act", lambda e: e.activation(out=Wb["tm1"][:], in_=Wb["tm1"][:], func=AF.Exp), reads=["tm1"], writes=["tm1"])
                                P.op("act", lambda e: e.activation(out=Wb["tm2"][:], in_=Wb["tm2"][:], func=AF.Exp), reads=["tm2"], writes=["tm2"])
                                P.op("pool", lambda e, Mc_s=Mc_s: e.tensor_tensor(out=Wb["Dcs"][:], in0=Wb["tm1"][:], in1=MK[:, Mc_s, :], op=ALU.mult), reads=["tm1", "MK"], writes=["Dcs"])
                                P.op("pool", lambda e, McT_i=McT_i: e.tensor_tensor(out=Wb["DT"][:], in0=Wb["tm2"][:], in1=MK[:, McT_i, :], op=ALU.mult), reads=["tm2", "MK"], writes=["DT"])
                                first = (s == "s" and d == 0 and t0 == 0 and n == 0 and h == 0)
                                dump(Wb["Dcs"][:], "Dcs", 128, first)
                                dump(Wb["DT"][:], "DT", 128, first)
                                pk = next_ps()
                                mm(psum[pk][:, 0:128], kpad[:, h, ts_], cv[:, 2 + h // 2, ts_], ["kpad", "cv%d" % (2 + h // 2)], pk)
                                P.op("dve", lambda e, pk=pk, b1=b1: e.scalar_tensor_tensor(out=Wb["A0"][:], in0=psum[pk][:, 0:128], scalar=b1, in1=Wb["Dcs"][:], op0=ALU.mult, op1=ALU.mult),
                                     reads=["ps%d" % pk, "gbeta", "Dcs"], writes=["A0"])
                                pq = next_ps()
                                mm(psum[pq][:, 0:128], kpad[:, h, ts_], cv[:, h // 2, ts_], ["kpad", "cv%d" % (h // 2)], pq)
                                P.op("dve", lambda e, pq=pq: e.tensor_tensor(out=Wb["qkT"][:], in0=psum[pq][:, 0:128], in1=Wb["DT"][:], op=ALU.mult), reads=["ps%d" % pq, "DT"], writes=["qkT"])
                                dump(Wb["A0"][:], "A0", 128, first)
                                dump(Wb["qkT"][:], "qkT", 128, first)
                                pt = next_ps()
                                P.op("pe", lambda e, pt=pt: e.transpose(psum[pt][:, 0:128], Wb["A0"][:], MK[:, IDF, :]), reads=["A0", "MK"], writes=["ps%d" % pt])
                                P.op("act", lambda e, pt=pt: e.copy(out=Wb["B0"][:], in_=psum[pt][:, 0:128]), reads=["ps%d" % pt], writes=["B0"])
                                P.op("pool", lambda e: e.tensor_tensor(out=Wb["T0"][:], in0=MK[:, IDF, :], in1=Wb["B0"][:], op=ALU.subtract), reads=["MK", "B0"], writes=["T0"])
                                ca, cb, ctt = "A0", "B0", "T0"
                                for lev in range(5):
                                    na, nb_, ntt = ("A1", "B1", "T1") if ca == "A0" else ("A0", "B0", "T0")
                                    p1_ = next_ps()
                                    mm(psum[p1_][:, 0:128], Wb[cb][:], Wb[ca][:], [ca, cb], p1_)
                                    P.op("act", lambda e, p1_=p1_, na=na: e.copy(out=Wb[na][:], in_=psum[p1_][:, 0:128]), reads=["ps%d" % p1_], writes=[na])
                                    if lev < 4:
                                        p2_ = next_ps()
                                        mm(psum[p2_][:, 0:128], Wb[ca][:], Wb[cb][:], [ca, cb], p2_)
                                        P.op("dve", lambda e, p2_=p2_, nb_=nb_: e.tensor_copy(out=Wb[nb_][:], in_=psum[p2_][:, 0:128]), reads=["ps%d" % p2_], writes=[nb_])
                                    p3_ = next_ps()
                                    mm(psum[p3_][:, 0:128], Wb[na][:], Wb[ctt][:], [na, ctt], p3_)
                                    P.op("dve", lambda e, p3_=p3_, ctt=ctt, ntt=ntt: e.tensor_tensor(out=Wb[ntt][:], in0=psum[p3_][:, 0:128], in1=Wb[ctt][:], op=ALU.add),
                                         reads=["ps%d" % p3_, ctt], writes=[ntt])
                                    ca, cb, ctt = na, nb_, ntt
                                Tt = Wb[ctt]
                                P.op("dve", lambda e, b1=b1, hs=hs, h=h: e.tensor_scalar(out=Sb["Vb"][:], in0=vtok[:, h // 2, hs], scalar1=b1, scalar2=None, op0=ALU.mult),
                                     reads=["vtok", "gbeta"], writes=["Vb"])
                                kb, kdk = "Kbg%d" % par_, "kd%d" % par_
                                P.op("dve", lambda e, hs=hs, h=h, kb=kb, dh=dh: e.tensor_scalar(out=Wb[kb][:, hs], in0=ktok[:, h // 2, hs], scalar1=tq["bge"][:, dh:dh + 1], scalar2=None, op0=ALU.mult),
                                     reads=["ktok", "bge"], writes=[kb])
                                P.op("dve", lambda e, hs=hs, h=h, kdk=kdk, dh=dh: e.tensor_scalar(out=Wb[kdk][:, hs], in0=ktok[:, h // 2, hs], scalar1=tq["ekd"][:, dh:dh + 1], scalar2=None, op0=ALU.mult),
                                     reads=["ktok", "ekd"], writes=[kdk])
                                pu = next_ps()
                                mm(psum[pu][:, 0:64], Tt[:], Sb["Vb"][:], [ctt, "Vb"], pu)
                                P.op("act", lambda e, pu=pu: e.copy(out=Sb["U"][:], in_=psum[pu][:, 0:64]), reads=["ps%d" % pu], writes=["U"])
                                pw = next_ps()
                                mm(psum[pw][:, 0:128], Wb[kb][:], Tt[:], [kb, ctt], pw)
                                P.op("act", lambda e, pw=pw: e.copy(out=Wb["WT"][:], in_=psum[pw][:, 0:128]), reads=["ps%d" % pw], writes=["WT"])
                                dump(Tt[:], ctt, 128, first)
                                dump(Sb["U"][:], "U", 64, first)
                                dump(Wb["WT"][:], "WT", 128, first)
                                dump(kpad[:, 0, 0:128], "kpad", 128, first)
                                dump(cv[:, 0, 0:128], "cv0", 128, first)
                                dump(cv[:, 4, 0:128], "cv4", 128, first)
                                dump(tq["gc"][:], "gc", 8, first)
                                dump(gt["beta"][:, 0, :], "gbeta", 8, first)
                                dump(gt["g"][:, 0, :], "gg", 8, first)
                                dump(vtok[:, 0, :], "vtok", 128, first)
                                for c_ in ((0, 1) if not rev else (1, 0)):
                                    cs = slice(c_ * 64, c_ * 64 + 64)
                                    vn = "vnA" if c_ == 0 else "vnB"
                                    Sv = Sst[:, h, :]
                                    pa_ = next_ps()
                                    mm(psum[pa_][:, 0:64], Wb["WT"][:], Sv, ["WT", "Sst"], pa_)
                                    P.op("dve", lambda e, pa_=pa_, cs=cs, vn=vn: e.tensor_tensor(out=Sb[vn][cs, :], in0=Sb["U"][cs, :], in1=psum[pa_][cs, 0:64], op=ALU.subtract),
                                         reads=["ps%d" % pa_, "U"], writes=[vn])
                                    pb_ = next_ps()
                                    mm(psum[pb_][:, 0:64], qpad[:, h, ts_], Sv, ["qpad", "Sst"], pb_)
                                    P.op("dve", lambda e, pb_=pb_, cs=cs, dh=dh: e.tensor_scalar(out=Sb["tt"][cs, :], in0=psum[pb_][cs, 0:64], scalar1=tq["eg"][cs, dh:dh + 1], scalar2=None, op0=ALU.mult),
                                         reads=["ps%d" % pb_, "eg"], writes=["tt"])
                                    pc_ = next_ps()
                                    mm(psum[pc_][:, 0:64], Wb["qkT"][:], Sb[vn][:], ["qkT", vn], pc_)
                                    P.op("dve", lambda e, pc_=pc_, cs=cs, h=h: e.tensor_tensor(out=otile[cs, h, :], in0=psum[pc_][cs, 0:64], in1=Sb["tt"][cs, :], op=ALU.add),
                                         reads=["ps%d" % pc_, "tt"], writes=["otile"])
                                    pe_ = next_ps()
                                    mm(psum[pe_][:, 0:64], Wb[kdk][:], Sb[vn][:], [kdk, vn], pe_)
                                    P.op("dve", lambda e, pe_=pe_, h=h, c_=c_, dh=dh: e.scalar_tensor_tensor(out=Sst[:, h, :], in0=Sst[:, h, :], scalar=tq["gl%d" % c_][:, dh:dh + 1], in1=psum[pe_][:, 0:64],
                                                                                                         op0=ALU.mult, op1=ALU.add), reads=["ps%d" % pe_, "Sst", "gl%d" % c_], writes=["Sst"])
                            if not rev:
                                P.dma("pool", "gof", ofwd[s][tg0:tg0 + 128, :], otile[:].rearrange("p h d -> p (h d)"), reads=["otile"], writes=["of_" + s])
                            else:
                                P.dma("sp", "gol", ofw[:].rearrange("p h d -> p (h d)"), ofwd[s][tg0:tg0 + 128, :], reads=["of_" + s], writes=["ofw"])
                                P.op("dve", lambda e: e.tensor_tensor(out=fin["o2"][:], in0=otile[:], in1=ofw[:], op=ALU.add), reads=["otile", "ofw"], writes=["o2"])
                                P.op("pool", lambda e: e.tensor_tensor(out=fin["on"][:], in0=fin["o2"][:], in1=fin["o2"][:], op=ALU.mult), reads=["o2"], writes=["on"])
                                P.op("dve", lambda e: e.reduce_sum(out=ssq[:, 0:4], in_=fin["on"][:], axis=mybir.AxisListType.X), reads=["on"], writes=["ssq"])
                                P.op("act", lambda e: e.activation(out=ssq[:, 0:4], in_=ssq[:, 0:4], func=AF.Sqrt, bias=epsb[:, 0:1], scale=1.0 / 64), reads=["ssq", "epsb"], writes=["ssq"])
                                P.op("dve", lambda e: e.reciprocal(out=ssq[:, 0:4], in_=ssq[:, 0:4]), reads=["ssq"], writes=["ssq"])
                                for h in range(4):
                                    P.op("dve", lambda e, h=h: e.scalar_tensor_tensor(out=fin["on"][:, h, :], in0=fin["o2"][:, h, :], scalar=ssq[:, h:h + 1], in1=gnl[:, h, :], op0=ALU.mult, op1=ALU.mult),
                                         reads=["o2", "ssq", "gn"], writes=["on"])
                                for c_ in range(2):
                                    pi = next_ps()
                                    P.op("pe", lambda e, pi=pi, c_=c_: e.transpose(psum[pi][:, 0:128], fin["on"][:, 2 * c_:2 * c_ + 2, :].rearrange("p h d -> p (h d)"), MK[:, IDF, :]),
                                         reads=["on", "MK"], writes=["ps%d" % pi])
                                    P.op("dve", lambda e, pi=pi, c_=c_, ts_=ts_: e.tensor_tensor(out=oT[c_][:], in0=psum[pi][:, 0:128], in1=zt[:, c_, ts_], op=ALU.mult),
                                         reads=["ps%d" % pi, "zt"], writes=["oT%d" % c_])
                                    P.dma("sp", "gos%d" % c_, obuf[s][c_ * 128:(c_ + 1) * 128, tg0:tg0 + 128], oT[c_][:], reads=["oT%d" % c_], writes=["ob_" + s])

        if not test_o and not {"attn", "s5", "gdn"} <= set(mixers):
            P.op("pool", lambda e: e.memset(ot[0][:], 0.0), writes=["ot0"])
            for s, L in seqs:
                for t0 in range(0, L, TB):
                    P.dma("sp", "z0", obuf[s][:, t0:t0 + TB].rearrange("(c p) n -> p c n", p=128), ot[0][:],
                          reads=["ot0"], writes=["ob_" + s])
        for l in range(depth):
            last = l == depth - 1
            if mixers:
                stage_A(l)
            if "attn" in mixers:
                attention(l)
            if "s5" in mixers:
                s5(l)
            if "gdn" in mixers:
                gdn(l)
            if test_mix:
                break
            apos[0] = ARENA
            P.barrier()
            OG, OB, OO = 0, 4 * 8 * D, 4 * 8 * D + 4 * 2 * D
            for b in range(4):
                load_w(OG + b * 8 * D, w_gate[l, b], D, D)
                load_w(OB + b * 2 * D, w_branch[l, b], 256, D)
            load_w(OO, w_out[l], D, D)
            for s, L in seqs:
                src = xin[s] if l == 0 else xbuf[s][1]
                for t0 in range(0, L, TB):
                    bi = (t0 // TB) % 2
                    xk, ok, xok = "xt%d" % bi, "ot%d" % bi, "xo"
                    P.dma("sp", "x%d" % bi, xt[bi][:], src[:, t0:t0 + TB].rearrange("(c p) n -> p c n", p=128), writes=[xk])
                    P.dma("act", "o%d" % bi, ot[bi][:], obuf[s][:, t0:t0 + TB].rearrange("(c p) n -> p c n", p=128),
                          reads=["ob_" + s], writes=[ok])
                    rmsnorm(xt[bi], xk, (2 * l) * 8, hT, "hT", TB)
                    for j in range(8):
                        for b in range(4):
                            pg, pb = next_ps(), next_ps()
                            for k in range(8):
                                P.op("pe", lambda e, pg=pg, k=k, b=b, j=j: e.matmul(
                                    psum[pg][:, :TB], lhsT=wres[:, OG + b * 8 * D + k * D + j * 128: OG + b * 8 * D + k * D + j * 128 + 128],
                                    rhs=hT[:, k, :], start=(k == 0), stop=(k == 7)),
                                    reads=["hT%d" % k, "wres"], writes=["ps%d" % pg])
                            for k in range(2):
                                P.op("pe", lambda e, pb=pb, k=k, b=b, j=j, bi=bi: e.matmul(
                                    psum[pb][:, :TB], lhsT=wres[:, OB + b * 2 * D + k * D + j * 128: OB + b * 2 * D + k * D + j * 128 + 128],
                                    rhs=ot[bi][:, 2 * b + k, :], start=(k == 0), stop=(k == 1)),
                                    reads=[ok, "wres"], writes=["ps%d" % pb])
                            si = b % 2
                            P.op("act", lambda e, pg=pg, si=si: e.activation(out=sig[si][:], in_=psum[pg][:, :TB], func=AF.Sigmoid),
                                 reads=["ps%d" % pg], writes=["sig%d" % si])
                            if b == 0:
                                P.op("dve", lambda e, pb=pb, si=si: e.tensor_tensor(out=macc[:], in0=psum[pb][:, :TB], in1=sig[si][:], op=ALU.mult),
                                     reads=["ps%d" % pb, "sig%d" % si], writes=["macc"])
                            else:
                                P.op("dve", lambda e, pb=pb, si=si: e.tensor_tensor(out=tmp[si][:], in0=psum[pb][:, :TB], in1=sig[si][:], op=ALU.mult),
                                     reads=["ps%d" % pb, "sig%d" % si], writes=["tmp%d" % si])
                                dst = mg[:, j, :] if b == 3 else macc[:]
                                P.op("pool", lambda e, si=si, dst=dst: e.tensor_tensor(out=dst, in0=macc[:], in1=tmp[si][:], op=ALU.add),
                                     reads=["macc", "tmp%d" % si], writes=["macc"] if b < 3 else ["mg%d" % j])
                    for j in range(8):
                        pi = next_ps()
                        for k in range(8):
                            P.op("pe", lambda e, pi=pi, k=k, j=j: e.matmul(
                                psum[pi][:, :TB], lhsT=wres[:, OO + k * D + j * 128: OO + k * D + j * 128 + 128], rhs=mg[:, k, :],
                                start=(k == 0), stop=(k == 7)), reads=["mg%d" % k, "wres"], writes=["ps%d" % pi])
                        P.op("dve", lambda e, pi=pi, j=j, bi=bi: e.tensor_tensor(out=xo[bi][:, j, :], in0=psum[pi][:, :TB], in1=xt[bi][:, j, :], op=ALU.add),
                             reads=["ps%d" % pi, xk], writes=[xok])
                    P.dma("pool", "xs%d" % bi, xbuf[s][0][:, t0:t0 + TB].rearrange("(c p) n -> p c n", p=128), xo[bi][:],
                          reads=[xok], writes=["xb0_" + s])
            HW = HF * 128
            FG, FU, FD = 0, 8 * HW, 16 * HW
            for hp in range(2):
                P.barrier()
                load_w(FG, w_fg[l][:, hp * HW:(hp + 1) * HW], D, HW)
                load_w(FU, w_fu[l][:, hp * HW:(hp + 1) * HW], D, HW)
                load_w(FD, w_fd[l][hp * HW:(hp + 1) * HW, :], HW, D)
                for s, L in seqs:
                    for t0 in range(0, L, TB):
                        bi = (t0 // TB) % 2
                        xk, xok, xrk = "xt%d" % bi, "xo", "xr"
                        blk = lambda ap: ap[:, t0:t0 + TB].rearrange("(c p) n -> p c n", p=128)
                        P.dma("sp", "x%d" % bi, xt[bi][:], blk(xbuf[s][0]), reads=["xb0_" + s], writes=[xk])
                        if hp == 1:
                            P.dma("act", "r%d" % bi, xr[bi][:], blk(xbuf[s][2]), reads=["xb2_" + s], writes=[xrk])
                        res_t, res_k = (xt[bi], xk) if hp == 0 else (xr[bi], xrk)
                        rmsnorm(xt[bi], xk, (2 * l + 1) * 8, hT, "hT", TB)
                        for j in range(HF):
                            pg, pu = next_ps(), next_ps()
                            for (pp, off) in ((pg, FG), (pu, FU)):
                                for k in range(8):
                                    P.op("pe", lambda e, pp=pp, off=off, k=k, j=j: e.matmul(
                                        psum[pp][:, :TB], lhsT=wres[:, off + k * HW + j * 128: off + k * HW + j * 128 + 128], rhs=hT[:, k, :],
                                        start=(k == 0), stop=(k == 7)), reads=["hT%d" % k, "wres"], writes=["ps%d" % pp])
                            si = j % 2
                            P.op("act", lambda e, pg=pg, si=si: e.activation(out=sig[si][:], in_=psum[pg][:, :TB], func=AF.Silu),
                                 reads=["ps%d" % pg], writes=["sig%d" % si])
                            P.op("dve", lambda e, pu=pu, si=si, j=j: e.tensor_tensor(out=act[:, j, :], in0=psum[pu][:, :TB], in1=sig[si][:], op=ALU.mult),
                                 reads=["ps%d" % pu, "sig%d" % si], writes=["act%d" % j])
                        for j in range(8):
                            pi = next_ps()
                            for k in range(HF):
                                P.op("pe", lambda e, pi=pi, k=k, j=j: e.matmul(
                                    psum[pi][:, :TB], lhsT=wres[:, FD + k * D + j * 128: FD + k * D + j * 128 + 128], rhs=act[:, k, :],
                                    start=(k == 0), stop=(k == HF - 1)), reads=["act%d" % k, "wres"], writes=["ps%d" % pi])
                            P.op("dve", lambda e, pi=pi, j=j, res_t=res_t, bi=bi: e.tensor_tensor(
                                out=xo[bi][:, j, :], in0=psum[pi][:, :TB], in1=res_t[:, j, :], op=ALU.add),
                                reads=["ps%d" % pi, res_k], writes=[xok])
                        if hp == 0:
                            P.dma("pool", "xs%d" % bi, blk(xbuf[s][2]), xo[bi][:], reads=[xok], writes=["xb2_" + s])
                        elif not last:
                            P.dma("pool", "xs%d" % bi, blk(xbuf[s][1]), xo[bi][:], reads=[xok], writes=["xb1_" + s, "xsrc_" + s])
                        else:
                            if dense_mode:
                                P.dma("sp", "xs2%d" % bi, blk(xbuf[s][1]), xo[bi][:], reads=[xok], writes=["xb1_" + s])
                            rmsnorm(xo[bi], xok, (2 * depth) * 8, xr[bi], xrk, TB, same=True)
                            P.dma("pool", "xs%d" % bi, blk(yout[s]), xr[bi][:], reads=[xrk], writes=["y_" + s])
        P.emit()
    return nc


def _norm_cols(v):
    return np.ascontiguousarray(v.reshape(8, 128).T)


def _consts():
    import ml_dtypes
    n = 8
    sl = 2.0 ** (-8.0 * np.arange(1, n + 1) / n)
    sl_d, sl_b = sl[:4], sl[4:]
    k = np.arange(128)[:, None]
    q = np.arange(128)[None, :]
    tabs = []
    for rr in range(3):
        rel = 128 * (rr - 1) + k - q
        tabs.append(np.concatenate([np.exp(-sl_d[h] * np.abs(rel)) * (np.abs(rel) <= 128) for h in range(4)], axis=1))
    for dil in (1, 4, 16):
        for rr in range(2):
            rel = 128 * rr - 64 + k - q
            tabs.append(np.concatenate([np.exp(-sl_b[h] * dil * np.abs(rel)) * (np.abs(rel) <= 64) for h in range(4)], axis=1))
    wtab = np.ascontiguousarray(np.concatenate(tabs, axis=1).astype(np.float32))
    ii, jj = np.arange(128)[:, None], np.arange(128)[None, :]
    same = (ii // 64) == (jj // 64)
    gm = [same & (ii >= jj), same & (ii > jj), same & (ii <= jj), same & (ii < jj), same,
          (ii // 64 == 0) & (jj >= 0), (ii // 64 == 1) & (jj >= 0), ii == jj]
    gmask = np.ascontiguousarray(np.concatenate([m.astype(np.float32) for m in gm], axis=1))
    ident = np.eye(128, dtype=np.float32).astype(ml_dtypes.bfloat16)
    return {"wtab": wtab, "ident": ident, "gmask": gmask}


def _host_inputs(inp):
    depth = inp["w_in"].shape[0]
    f = lambda a: np.ascontiguousarray(np.asarray(a, dtype=np.float32))
    norms = np.concatenate([_norm_cols(f(inp[n][l])) for l in range(depth) for n in ("norm1", "norm2")]
                           + [_norm_cols(f(inp["final_norm"]))], axis=1)
    shared = {"w_in": f(inp["w_in"]), "w_gate": f(inp["w_gate"]), "w_branch": f(inp["w_branch"]), "w_out": f(inp["w_out"]),
              "w_ffn_gate": f(inp["w_ffn_gate"]), "w_ffn_up": f(inp["w_ffn_up"]), "w_ffn_down": f(inp["w_ffn_down"]),
              "norms": np.ascontiguousarray(norms)}
    shared.update(_consts())
    depth_ = depth
    lam_re, lam_im, lstep = f(inp["s5_lam_re"]), f(inp["s5_lam_im"]), f(inp["s5_log_step"])
    par = np.zeros((128, depth_, 2, 8, 3), np.float32)
    for g2 in range(2):
        par[g2 * 64:(g2 + 1) * 64, :, :, :, 0] = np.transpose(lam_re[:, :, g2::2, :], (3, 0, 1, 2))
        par[g2 * 64:(g2 + 1) * 64, :, :, :, 1] = np.transpose(lam_im[:, :, g2::2, :], (3, 0, 1, 2))
        par[g2 * 64:(g2 + 1) * 64, :, :, :, 2] = lstep[None, :, :, g2::2]
    shared["s5par"] = np.ascontiguousarray(par.reshape(128, depth_ * 48))
    b_re, b_im, c_re, c_im = f(inp["s5_b_re"]), f(inp["s5_b_im"]), f(inp["s5_c_re"]), f(inp["s5_c_im"])
    Bl = np.zeros((depth_, 128, 8, 2, 128), np.float32)
    Cl = np.zeros((depth_, 128, 2, 8, 2, 128), np.float32)
    for g in range(16):
        j, g2, r0 = g // 2, g % 2, 16 * (g % 8)
        for c, bb in enumerate((b_re, b_im)):
            Bl[:, r0:r0 + 16, j, c, g2 * 64:(g2 + 1) * 64] = np.transpose(bb[:, g], (0, 2, 1))
        for c, cc in enumerate((c_re, c_im)):
            Cl[:, g2 * 64:(g2 + 1) * 64, :, j, c, r0:r0 + 16] = np.transpose(cc[:, :, g], (0, 3, 1, 2))
    shared["s5B"] = np.ascontiguousarray(Bl.reshape(depth_, 128, -1))
    shared["s5C"] = np.ascontiguousarray(Cl.reshape(depth_, 128, -1))
    shared["s5d"] = np.ascontiguousarray(np.transpose(f(inp["s5_d"]).reshape(depth_, 2, 128), (2, 0, 1)).reshape(128, depth_ * 2))
    shared["s5_w_glu"] = f(inp["s5_w_glu"])
    conv = f(inp["conv_a"])
    shared["gconv"] = np.ascontiguousarray(np.transpose(conv.reshape(depth_, 5, 6, 128), (3, 0, 2, 1)).reshape(128, depth_ * 30))
    gp_ = np.concatenate([f(inp["a_log"]).reshape(depth_, 8), f(inp["dt_bias"]).reshape(depth_, 8)], axis=1)
    shared["gpar"] = np.ascontiguousarray(np.broadcast_to(gp_.reshape(1, depth_ * 16), (128, depth_ * 16)))
    gn_ = np.broadcast_to(f(inp["a_norm"]).reshape(depth_, 1, 64), (depth_, 4, 64)).reshape(1, depth_ * 256)
    shared["gnorm"] = np.ascontiguousarray(np.broadcast_to(gn_, (128, depth_ * 256)))
    shared["sinkb"] = np.ascontiguousarray(np.broadcast_to(f(inp["attn_sink"]).reshape(1, -1), (64, depth * 4)))
    return shared


_CACHE = {}
_MIX = ("attn", "s5", "gdn")


def _layer_inputs(shared, l, depth):
    m = {}
    for k in ("w_in", "w_gate", "w_branch", "w_out", "w_ffn_gate", "w_ffn_up", "w_ffn_down", "s5B", "s5C", "s5_w_glu"):
        m[k] = np.ascontiguousarray(shared[k][l:l + 1])
    nr = shared["norms"]
    m["norms"] = np.ascontiguousarray(np.concatenate([nr[:, 16 * l:16 * l + 16], nr[:, 16 * depth:16 * depth + 8]], axis=1))
    for k, w in (("sinkb", 4), ("s5par", 48), ("s5d", 2), ("gconv", 30), ("gpar", 16), ("gnorm", 256)):
        m[k] = np.ascontiguousarray(shared[k][:, w * l:w * (l + 1)])
    for k in ("wtab", "ident", "gmask"):
        m[k] = shared[k]
    return m


def kernel(**inp):
    LS, LP = inp["x_sample"].shape[1], inp["x_prompt"].shape[1]
    depth = inp["w_in"].shape[0]
    key = (LS, LP)
    if key not in _CACHE:
        _CACHE[key] = (build(LS, LP, 1, test_mix=True, lean=True, mixers=_MIX), build(LS, LP, 1, test_o=True, dense_mode=True))
    mixP, denseP = _CACHE[key]
    f = lambda a: np.ascontiguousarray(np.asarray(a, dtype=np.float32))
    shared = _host_inputs(inp)
    nb = inp["x_sample"].shape[0]
    xp = np.ascontiguousarray(f(inp["x_prompt"])[0].T)
    xs_all = f(inp["x_sample"])
    cur = [{"x_p": xp, "x_s": np.ascontiguousarray(xs_all[c % nb].T)} for c in range(8)]
    res = None
    mix_keys = ("w_in", "norms", "wtab", "ident", "sinkb", "s5par", "s5B", "s5C", "s5d", "s5_w_glu", "gmask", "gconv", "gpar", "gnorm")
    for l in range(depth):
        lay = _layer_inputs(shared, l, depth)
        in_maps = [dict({k: lay[k] for k in mix_keys}, **cur[c]) for c in range(8)]
        rm = run_bass_kernel_spmd(mixP, in_maps, core_ids=list(range(8)))
        in_maps = [dict(lay, **cur[c], ob_s=rm.results[c]["ob_s"], ob_p=rm.results[c]["ob_p"]) for c in range(8)]
        res = run_bass_kernel_spmd(denseP, in_maps, core_ids=list(range(8)))
        cur = [{"x_p": res.results[c]["xb1_p"], "x_s": res.results[c]["xb1_s"]} for c in range(8)]
    y_p = np.ascontiguousarray(res.results[0]["y_p"].T)[None]
    y_s = np.stack([np.ascontiguousarray(res.results[c]["y_s"].T) for c in range(nb)], axis=0)
    return (y_p.astype(np.float32), y_s.astype(np.float32))
```
